# Optimizing a Trainium2 kernel written in Bass

```python
import jax
import jax.numpy as jnp
from jax import lax
import numpy as np

D_MODEL = 1024
BATCH = 8
SEQ = 2048
DEPTH = 2

GRID_W = 64
CTX_LEN = 256
EPS = 1e-6
ROPE_BASE = 10000.0

GLA_HEADS = 4
GLA_DK = 128
GLA_DV = 256
GLA_GATE_RANK = 16
GLA_GATE_NORM = 16.0
GLA_CHUNK = 64
SWA_HEADS = 16
SWA_KV_HEADS = 2
SWA_GROUP = SWA_HEADS // SWA_KV_HEADS
SWA_HEAD_DIM = 64
WINDOW = 128
SWA_BLOCK = 128
MLA_HEADS = 8
MLA_Q_RANK = 384
MLA_KV_RANK = 256
MLA_NOPE = 128
MLA_ROPE = 64
MLA_V = 128
MLA_BLOCK = 128
D_FF = -(-(8 * D_MODEL) // (3 * 256)) * 256

IN_SPLITS = (
    GLA_HEADS * GLA_DK, GLA_HEADS * GLA_DK, GLA_HEADS * GLA_DV, GLA_HEADS * GLA_DV,
    GLA_GATE_RANK, GLA_GATE_RANK,
    SWA_HEADS * SWA_HEAD_DIM, SWA_KV_HEADS * SWA_HEAD_DIM, SWA_KV_HEADS * SWA_HEAD_DIM,
    MLA_Q_RANK, MLA_KV_RANK, MLA_ROPE,
    3 * D_MODEL,
)
D_IN = sum(IN_SPLITS)

kernel_name = 'hybrid_gla_swa_mla_dit_trunk'


def rmsnorm(x, g):
    x32 = x.astype(jnp.float32)
    y = x32 * lax.rsqrt(jnp.mean(x32 * x32, axis=-1, keepdims=True) + EPS)
    return (y * g.astype(jnp.float32)).astype(x.dtype)


def modulate(x, shift, scale):
    return x * (1 + scale) + shift


def split_cols(z):
    out, idx = [], 0
    for n in IN_SPLITS:
        out.append(z[..., idx:idx + n])
        idx += n
    return out


def to_heads(z, n):
    b_, t_, _ = z.shape
    return z.reshape(b_, t_, n, -1).transpose(0, 2, 1, 3)


def from_heads(z):
    b_, n, t_, d = z.shape
    return z.transpose(0, 2, 1, 3).reshape(b_, t_, n * d)


def grid_positions(n_tokens):
    rows = n_tokens // GRID_W
    row = jnp.broadcast_to(jnp.arange(rows, dtype=jnp.int32)[:, None], (rows, GRID_W)).reshape(-1)
    col = jnp.broadcast_to(jnp.arange(GRID_W, dtype=jnp.int32)[None, :], (rows, GRID_W)).reshape(-1)
    return row, col


def rope_1d(x, pos):
    half = x.shape[-1] // 2
    inv = jnp.power(ROPE_BASE, -jnp.arange(half, dtype=jnp.float32) / half)
    ang = pos.astype(jnp.float32)[:, None] * inv[None, :]
    cos, sin = jnp.cos(ang), jnp.sin(ang)
    x1 = x[..., :half].astype(jnp.float32)
    x2 = x[..., half:].astype(jnp.float32)
    return jnp.concatenate([x1 * cos - x2 * sin, x2 * cos + x1 * sin], axis=-1).astype(x.dtype)


def rope_2d(x, row, col):
    h = x.shape[-1] // 2
    return jnp.concatenate([rope_1d(x[..., :h], row), rope_1d(x[..., h:], col)], axis=-1)


def gla_chunked(q, k, v, log_a, s0, strict):
    b_, h_, t_, _ = q.shape
    dv = v.shape[-1]
    n = t_ // GLA_CHUNK
    ch = lambda z: z.astype(jnp.float32).reshape(b_, h_, n, GLA_CHUNK, z.shape[-1])
    q, k, v, log_a = ch(q), ch(k), ch(v), ch(log_a)
    cum = jnp.cumsum(log_a, axis=3)
    last = cum[:, :, :, -1:, :]
    q_dec = q * jnp.exp(cum)
    k_inv = k * jnp.exp(-cum)
    k_end = k * jnp.exp(last - cum)
    mask = jnp.tril(jnp.ones((GLA_CHUNK, GLA_CHUNK), dtype=bool), k=-1 if strict else 0)
    scores = jnp.where(mask, jnp.einsum('bhncd,bhnsd->bhncs', q_dec, k_inv), 0.0)
    o_intra = jnp.einsum('bhncs,bhnsv->bhncv', scores, v)
    kv_add = jnp.einsum('bhnsd,bhnsv->bhndv', k_end, v)
    decay = jnp.exp(last[:, :, :, 0, :])

    def step(state, xs):
        q_c, kv_c, dec_c = xs
        o_c = jnp.einsum('bhcd,bhdv->bhcv', q_c, state)
        return dec_c[..., None] * state + kv_c, o_c

    xs = (jnp.moveaxis(q_dec, 2, 0), jnp.moveaxis(kv_add, 2, 0), jnp.moveaxis(decay, 2, 0))
    s_fin, o_inter = lax.scan(step, s0.astype(jnp.float32), xs)
    o = o_intra + jnp.moveaxis(o_inter, 0, 2)
    return o.reshape(b_, h_, t_, dv), s_fin


def gla_bidir(q, k, v, la_f, la_b, s0_f, s0_b):
    flip = lambda z: jnp.flip(z, axis=2)
    o_f, s_f = gla_chunked(q, k, v, la_f, s0_f, strict=False)
    o_b, s_b = gla_chunked(flip(q), flip(k), flip(v), flip(la_b), s0_b, strict=True)
    return o_f + flip(o_b), s_f, s_b


def sink_attend(q, k, v, mask, sink):
    scale = q.shape[-1] ** -0.5
    s = jnp.einsum('bgrqd,bgkd->bgrqk', q, k, preferred_element_type=jnp.float32) * scale
    s = jnp.where(mask, s, -jnp.inf)
    sk = sink.astype(jnp.float32)[None, :, :, None, None]
    m = jnp.maximum(jnp.max(s, axis=-1, keepdims=True), sk)
    p = jnp.exp(s - m)
    den = jnp.sum(p, axis=-1, keepdims=True) + jnp.exp(sk - m)
    o = jnp.einsum('bgrqk,bgkd->bgrqd', p, v.astype(jnp.float32)) / den
    return o.astype(q.dtype)


def swa_latent(q, k, v, kc, vc, sink):
    b_, g_, r_, t_, d = q.shape
    nb = t_ // SWA_BLOCK
    span = 3 * SWA_BLOCK
    pad = ((0, 0), (0, 0), (SWA_BLOCK, SWA_BLOCK), (0, 0))
    kp, vp = jnp.pad(k, pad), jnp.pad(v, pad)
    off = jnp.arange(span) - SWA_BLOCK
    rel = off[None, :] - jnp.arange(SWA_BLOCK)[:, None]
    ctx_ok = jnp.ones((SWA_BLOCK, kc.shape[2]), dtype=bool)
    q_blocks = jnp.moveaxis(q.reshape(b_, g_, r_, nb, SWA_BLOCK, d), 3, 0)

    def one_block(args):
        i, q_i = args
        start = i * SWA_BLOCK
        k_i = jnp.concatenate([lax.dynamic_slice_in_dim(kp, start, span, axis=2), kc], axis=2)
        v_i = jnp.concatenate([lax.dynamic_slice_in_dim(vp, start, span, axis=2), vc], axis=2)
        key_pos = start + off
        band = (jnp.abs(rel) <= WINDOW) & ((key_pos >= 0) & (key_pos < t_))[None, :]
        return sink_attend(q_i, k_i, v_i, jnp.concatenate([band, ctx_ok], axis=1), sink)

    o = lax.map(one_block, (jnp.arange(nb), q_blocks))
    return jnp.moveaxis(o, 0, 3).reshape(b_, g_, r_, t_, d)


def mla_attend(qn, qr, kn, kr, v):
    b_, h_, t_, _ = qn.shape
    nb = t_ // MLA_BLOCK
    scale = (MLA_NOPE + MLA_ROPE) ** -0.5
    blocks = lambda z: jnp.moveaxis(z.reshape(b_, h_, nb, MLA_BLOCK, z.shape[-1]), 2, 0)

    def one_block(args):
        qn_i, qr_i = args
        s = jnp.einsum('bhqd,bhkd->bhqk', qn_i, kn, preferred_element_type=jnp.float32)
        s = s + jnp.einsum('bhqd,bkd->bhqk', qr_i, kr, preferred_element_type=jnp.float32)
        p = jax.nn.softmax(s * scale, axis=-1)
        return jnp.einsum('bhqk,bhkd->bhqd', p.astype(v.dtype), v)

    o = lax.map(one_block, (blocks(qn), blocks(qr)))
    return jnp.moveaxis(o, 0, 2).reshape(b_, h_, t_, v.shape[-1])


def token_mixer(h, hc, row, col, with_ctx_out, w_in, w_gk_fwd, b_gk_fwd, w_gk_bwd, b_gk_bwd,
                gla_norm, sinks, q_norm, w_q_up, kv_norm, w_kv_up, w_pa, w_pb, w_pc, w_o):
    (qa, ka, va, ga, gkf, gkb, qs, ks, vs, cq, ckv, kr, mg) = split_cols(h @ w_in)
    (qa_c, ka_c, va_c, ga_c, gkf_c, gkb_c, qs_c, ks_c, vs_c, cq_c, ckv_c, kr_c, mg_c) = split_cols(hc @ w_in)

    def gla_prep(q, k, v, gf, gb):
        la_f = jax.nn.log_sigmoid(gf @ w_gk_fwd + b_gk_fwd) / GLA_GATE_NORM
        la_b = jax.nn.log_sigmoid(gb @ w_gk_bwd + b_gk_bwd) / GLA_GATE_NORM
        return (to_heads(q, GLA_HEADS) * GLA_DK ** -0.5, to_heads(k, GLA_HEADS), to_heads(v, GLA_HEADS),
                to_heads(la_f, GLA_HEADS), to_heads(la_b, GLA_HEADS))

    def gla_post(o, g):
        return from_heads(rmsnorm(o, gla_norm)).astype(g.dtype) * jax.nn.silu(g)

    zero = jnp.zeros((hc.shape[0], GLA_HEADS, GLA_DK, GLA_DV), jnp.float32)
    o_a_c, s_f, s_b = gla_bidir(*gla_prep(qa_c, ka_c, va_c, gkf_c, gkb_c), zero, zero)
    o_a, _, _ = gla_bidir(*gla_prep(qa, ka, va, gkf, gkb), s_f, s_b)

    def swa_prep(q, k, v, rotate):
        q, k, v = to_heads(q, SWA_HEADS), to_heads(k, SWA_KV_HEADS), to_heads(v, SWA_KV_HEADS)
        if rotate:
            q, k = rope_2d(q, row, col), rope_2d(k, row, col)
        b_, _, t_, d = q.shape
        return q.reshape(b_, SWA_KV_HEADS, SWA_GROUP, t_, d), k, v

    def swa_post(o):
        b_, g_, r_, t_, d = o.shape
        return from_heads(o.reshape(b_, g_ * r_, t_, d))

    sink = sinks.reshape(SWA_KV_HEADS, SWA_GROUP)
    q_b, k_b, v_b = swa_prep(qs, ks, vs, True)
    q_bc, k_bc, v_bc = swa_prep(qs_c, ks_c, vs_c, False)
    o_b = swa_latent(q_b, k_b, v_b, k_bc, v_bc, sink)

    def mla_prep(cq_, ckv_, kr_, rotate):
        qf = to_heads(rmsnorm(cq_, q_norm) @ w_q_up, MLA_HEADS)
        kvf = to_heads(rmsnorm(ckv_, kv_norm) @ w_kv_up, MLA_HEADS)
        qn_, qr_ = qf[..., :MLA_NOPE], qf[..., MLA_NOPE:]
        kn_, v_ = kvf[..., :MLA_NOPE], kvf[..., MLA_NOPE:]
        if rotate:
            qr_, kr_ = rope_2d(qr_, row, col), rope_2d(kr_, row, col)
        return qn_, qr_, kn_, kr_, v_

    qn, qr, kn, kro, vm = mla_prep(cq, ckv, kr, True)
    qn_c, qr_c, kn_c, kro_c, vm_c = mla_prep(cq_c, ckv_c, kr_c, False)
    o_c = mla_attend(qn, qr, jnp.concatenate([kn, kn_c], axis=2), jnp.concatenate([kro, kro_c], axis=1),
                     jnp.concatenate([vm, vm_c], axis=2))

    def merge(y_a, y_b, y_c, gates):
        g_a, g_b, g_c = jnp.split(jax.nn.sigmoid(gates), 3, axis=-1)
        return (g_a * (y_a @ w_pa) + g_b * (y_b @ w_pb) + g_c * (y_c @ w_pc)) @ w_o

    y = merge(gla_post(o_a, ga), swa_post(o_b), from_heads(o_c), mg)
    if not with_ctx_out:
        return y, None
    l_c = hc.shape[1]
    o_b_c = sink_attend(q_bc, k_bc, v_bc, jnp.ones((l_c, l_c), dtype=bool), sink)
    o_c_c = mla_attend(qn_c, qr_c, kn_c, kro_c, vm_c)
    y_ctx = merge(gla_post(o_a_c, ga_c), swa_post(o_b_c), from_heads(o_c_c), mg_c)
    return y, y_ctx


def swiglu(h, w_in, w_out):
    gate, up = jnp.split(h @ w_in, 2, axis=-1)
    return (jax.nn.silu(gate) * up) @ w_out


def setup_inputs(seed: int = 0) -> dict:
    key = jax.random.key(seed)
    keys = iter(jax.random.split(key, 32))
    f32 = jnp.float32

    def w(shape, fan_in, gain=1.0):
        return jax.random.normal(next(keys), shape, f32) * (gain * fan_in ** -0.5)

    def norm_gain(shape):
        return 1.0 + 0.02 * jax.random.normal(next(keys), shape, f32)

    def small(shape, s):
        return s * jax.random.normal(next(keys), shape, f32)

    L = DEPTH
    return {
        'x': jax.random.normal(next(keys), (BATCH, SEQ, D_MODEL), f32),
        'c': jax.random.normal(next(keys), (BATCH, D_MODEL), f32),
        'ctx': jax.random.normal(next(keys), (BATCH, CTX_LEN, D_MODEL), f32),
        'c_ctx': jax.random.normal(next(keys), (D_MODEL,), f32),
        'w_mod': w((L, D_MODEL, 6 * D_MODEL), D_MODEL, 0.5),
        'b_mod': small((L, 6 * D_MODEL), 0.02),
        'norm_mix': norm_gain((L, D_MODEL)),
        'w_in': w((L, D_MODEL, D_IN), D_MODEL),
        'w_gk_fwd': w((L, GLA_GATE_RANK, GLA_HEADS * GLA_DK), GLA_GATE_RANK),
        'b_gk_fwd': small((L, GLA_HEADS * GLA_DK), 0.1),
        'w_gk_bwd': w((L, GLA_GATE_RANK, GLA_HEADS * GLA_DK), GLA_GATE_RANK),
        'b_gk_bwd': small((L, GLA_HEADS * GLA_DK), 0.1),
        'gla_norm': norm_gain((L, GLA_DV)),
        'sinks': small((L, SWA_HEADS), 0.5),
        'q_norm': norm_gain((L, MLA_Q_RANK)),
        'w_q_up': w((L, MLA_Q_RANK, MLA_HEADS * (MLA_NOPE + MLA_ROPE)), MLA_Q_RANK),
        'kv_norm': norm_gain((L, MLA_KV_RANK)),
        'w_kv_up': w((L, MLA_KV_RANK, MLA_HEADS * (MLA_NOPE + MLA_V)), MLA_KV_RANK),
        'w_pa': w((L, GLA_HEADS * GLA_DV, D_MODEL), GLA_HEADS * GLA_DV),
        'w_pb': w((L, SWA_HEADS * SWA_HEAD_DIM, D_MODEL), SWA_HEADS * SWA_HEAD_DIM),
        'w_pc': w((L, MLA_HEADS * MLA_V, D_MODEL), MLA_HEADS * MLA_V),
        'w_o': w((L, D_MODEL, D_MODEL), D_MODEL),
        'norm_ffn': norm_gain((L, D_MODEL)),
        'w_ffn_in': w((L, D_MODEL, 2 * D_FF), D_MODEL),
        'w_ffn_out': w((L, D_FF, D_MODEL), D_FF),
        'final_norm': norm_gain((D_MODEL,)),
    }


def reference(x, c, ctx, c_ctx, w_mod, b_mod, norm_mix, w_in, w_gk_fwd, b_gk_fwd, w_gk_bwd, b_gk_bwd,
              gla_norm, sinks, q_norm, w_q_up, kv_norm, w_kv_up, w_pa, w_pb, w_pc, w_o,
              norm_ffn, w_ffn_in, w_ffn_out, final_norm):
    row, col = grid_positions(x.shape[1])
    xc = ctx
    for l in range(DEPTH):
        last = l == DEPTH - 1
        mod = jax.nn.silu(c) @ w_mod[l] + b_mod[l]
        mod_c = jax.nn.silu(c_ctx) @ w_mod[l] + b_mod[l]
        sh1, sc1, g1, sh2, sc2, g2 = [m[:, None, :] for m in jnp.split(mod, 6, axis=-1)]
        sh1c, sc1c, g1c, sh2c, sc2c, g2c = jnp.split(mod_c, 6, axis=-1)
        h = modulate(rmsnorm(x, norm_mix[l]), sh1, sc1)
        hc = modulate(rmsnorm(xc, norm_mix[l]), sh1c, sc1c)
        y, y_ctx = token_mixer(h, hc, row, col, not last, w_in[l], w_gk_fwd[l], b_gk_fwd[l],
                               w_gk_bwd[l], b_gk_bwd[l], gla_norm[l], sinks[l], q_norm[l], w_q_up[l],
                               kv_norm[l], w_kv_up[l], w_pa[l], w_pb[l], w_pc[l], w_o[l])
        x = x + g1 * y
        x = x + g2 * swiglu(modulate(rmsnorm(x, norm_ffn[l]), sh2, sc2), w_ffn_in[l], w_ffn_out[l])
        if not last:
            xc = xc + g1c * y_ctx
            xc = xc + g2c * swiglu(modulate(rmsnorm(xc, norm_ffn[l]), sh2c, sc2c), w_ffn_in[l], w_ffn_out[l])
    return rmsnorm(x, final_norm)
```

```python
import numpy as np
from contextlib import ExitStack
import concourse.bass as bass
import concourse.mybir as mybir
from concourse.bass_utils import run_bass_kernel_spmd

F32 = mybir.dt.float32
BF16 = mybir.dt.bfloat16
AF = mybir.ActivationFunctionType
ALU = mybir.AluOpType

NL = 2
D = 1024
T = 2048
LC = 256
TT = T + LC
EPS = 1e-6
D_FF = 2816
OFF_QA, OFF_KA, OFF_VA, OFF_GA, OFF_GKF, OFF_GKB = 0, 512, 1024, 2048, 3072, 3088
OFF_QS, OFF_KS, OFF_VS = 3104, 4128, 4256
OFF_CQ, OFF_CKV, OFF_KR, OFF_MG = 4384, 4768, 5024, 5088
D_IN = 8160
TILES = [(0, 512), (512, 512), (1024, 512), (1536, 512), (2048, 256)]
V_BMOD, V_NMIX, V_NFFN, V_FIN, V_QN, V_KVN, V_GN, V_SINK, NV = 0, 48, 56, 64, 72, 75, 77, 79, 87


class Sync:
    ROT = 30000

    def __init__(self, nc, n_dma_sems=24):
        self.nc = nc
        self.eng = {'pe': nc.tensor, 'act': nc.scalar, 'dve': nc.vector, 'pool': nc.gpsimd, 'sp': nc.sync}
        self.sem, self.cnt, self.gen = {}, {}, {}
        for e in ('pe', 'act', 'dve', 'pool'):
            self.gen[e] = 0
            self.sem[e] = nc.alloc_semaphore(f"s_{e}_0")
            self.cnt[e] = 0
        self.dsem = [nc.alloc_semaphore(f"s_dma_{i}") for i in range(n_dma_sems)]
        self.dcnt = [0] * n_dma_sems
        self.dnext = 0
        self.seen = {e: {} for e in self.eng}
        self.res = {}
        self.n_wait = 0
        self.n_ins = 0

    def _need(self, e, ticket, out):
        if ticket is None:
            return
        sem, val = ticket
        k = sem.num
        if e == 'pe' and k == self.sem['pe'].num:
            return
        if self.seen[e].get(k, 0) >= val:
            return
        if k not in out or out[k][1] < val:
            out[k] = (sem, val)

    def _emit_waits(self, e, need):
        for k, (sem, val) in need.items():
            self.eng[e].wait_ge(sem, val)
            self.seen[e][k] = val
            self.n_wait += 1

    def _deps(self, e, reads, writes):
        need = {}
        for r in reads:
            st = self.res.get(r)
            if st is not None:
                self._need(e, st['w'], need)
        for w in writes:
            st = self.res.get(w)
            if st is not None:
                if st['r']:
                    for t in st['r'].values():
                        self._need(e, t, need)
                else:
                    self._need(e, st['w'], need)
        self._emit_waits(e, need)

    def _mark(self, ticket, reads, writes):
        for r in reads:
            st = self.res.setdefault(r, {'w': None, 'r': {}})
            st['r'][ticket[0].num] = ticket
        for w in writes:
            self.res[w] = {'w': ticket, 'r': {}}

    def _signal(self, e, ins):
        if self.cnt[e] >= self.ROT:
            self.gen[e] += 1
            self.sem[e] = self.nc.alloc_semaphore(f"s_{e}_{self.gen[e]}")
            self.cnt[e] = 0
        self.cnt[e] += 1
        ins.then_inc(self.sem[e], 1)
        return (self.sem[e], self.cnt[e])

    def op(self, e, fn, reads=(), writes=()):
        self._deps(e, reads, writes)
        ins = fn(self.eng[e])
        self.n_ins += 1
        t = self._signal(e, ins)
        self._mark(t, reads, writes)
        return t

    def mm(self, items, reads=(), writes=(), cont=()):
        self._deps('pe', reads, writes)
        pe = self.eng['pe']
        ins = None
        for (o, l, r, st, sp) in items:
            ins = pe.matmul(o, l, r, start=st, stop=sp)
            self.n_ins += 1
        t = self._signal('pe', ins)
        self._mark(t, reads, list(writes) + list(cont))
        return t

    def dma(self, q, out, in_, reads=(), writes=()):
        i = self.dnext
        self.dnext = (self.dnext + 1) % len(self.dsem)
        sem = self.dsem[i]
        need = {}
        if self.dcnt[i] > 0:
            self._need(q, (sem, self.dcnt[i]), need)
        self._emit_waits(q, need)
        self._deps(q, reads, writes)
        self.dcnt[i] += 16
        self.eng[q].dma_start(out=out, in_=in_).then_inc(sem, 16)
        self.n_ins += 1
        t = (sem, self.dcnt[i])
        self._mark(t, reads, writes)
        return t

    def barrier(self):
        for e in self.eng:
            need = {}
            for o in ('pe', 'act', 'dve', 'pool'):
                if self.cnt[o] > 0:
                    sem, val = self.sem[o], self.cnt[o]
                    if self.seen[e].get(sem.num, 0) < val:
                        need[sem.num] = (sem, val)
            for i, s in enumerate(self.dsem):
                if self.dcnt[i] > 0 and self.seen[e].get(s.num, 0) < self.dcnt[i]:
                    need[s.num] = (s, self.dcnt[i])
            self._emit_waits(e, need)
        self.res.clear()


def build(n_layers=NL, dbg=None):
    nc = bass.Bass("TRN2", target_bir_lowering=False)

    def din(name, shape, dt=F32):
        return nc.dram_tensor(name, shape, dt, kind="ExternalInput")

    xT_in = din("xT", [D, T])
    ctxT_in = din("ctxT", [D, LC])
    cc_in = din("cc", [128, 8, 2])
    vecs_in = din("vecs", [128, NL, NV])
    rope_in = din("rope", [128, 2, T])
    tri_in = din("tri", [128, 6, 128])
    w_mod = din("w_mod", [NL, D, 6 * D])
    w_in = din("w_in", [NL, D, D_IN])
    w_gkf = din("w_gk_fwd", [NL, 16, 512])
    b_gkf = din("b_gk_fwd", [NL, 512])
    w_gkb = din("w_gk_bwd", [NL, 16, 512])
    b_gkb = din("b_gk_bwd", [NL, 512])
    w_q_up = din("w_q_up", [NL, 384, 1536])
    w_kv_up = din("w_kv_up", [NL, 256, 2048])
    w_pa = din("w_pa", [NL, D, D])
    w_pb = din("w_pb", [NL, D, D])
    w_pc = din("w_pc", [NL, D, D])
    w_o = din("w_o", [NL, D, D])
    w_fi = din("w_ffn_in", [NL, D, 2 * D_FF])
    w_fo = din("w_ffn_out", [NL, D_FF, D])
    outT = nc.dram_tensor("outT", [D, T], F32, kind="ExternalOutput")
    xs = nc.dram_tensor("xs_scr", [D, T], F32)
    mD = nc.dram_tensor("m_scr", [D, TT], F32)
    import os as _os
    yD = nc.dram_tensor("y_scr", [D, TT], BF16, kind=("ExternalOutput" if _os.environ.get("KDUMP") else "Internal"))
    dbg_t = {}
    if dbg:
        for name, (shape, dt) in dbg.items():
            dbg_t[name] = nc.dram_tensor("dbg_" + name, shape, dt, kind="ExternalOutput")

    def fm(ap):
        return ap.rearrange("(k p) t -> p k t", p=128)

    xs_v, mD_v, yD_v, outT_v = fm(xs.ap()), fm(mD.ap()), fm(yD.ap()), fm(outT.ap())
    w_in_v = [fm(w_in[l]) for l in range(NL)]

    S = Sync(nc)
    uid = [0]

    def T_(es, shape, dt, name="t"):
        uid[0] += 1
        return es.enter_context(nc.sbuf_tensor(f"{name}_{uid[0]}", shape, dt))

    PS = nc.alloc_psum_tensor("PS", [128, 8, 512], F32)

    G = ExitStack()
    vecs = T_(G, [128, NL, NV], F32, "vecs")
    cc = T_(G, [128, 8, 2], F32, "cc")
    xcT = T_(G, [128, 8, LC], F32, "xcT")
    MODT = [T_(G, [128, 48, 2], F32, f"modT{i}") for i in range(2)]
    A1S = [T_(G, [128, 8, 2], F32, f"A1{i}") for i in range(2)]
    A2S = [T_(G, [128, 8, 2], F32, f"A2{i}") for i in range(2)]
    ones1024 = T_(G, [128, 128], F32, "o1024")
    ones384 = T_(G, [128, 128], F32, "o384")
    ones256 = T_(G, [128, 128], F32, "o256")
    onesb = T_(G, [128, 128], BF16, "onesb")
    S.dma('sp', vecs[:], vecs_in.ap(), writes=['vecs'])
    S.dma('sp', cc[:], cc_in.ap(), writes=['cc'])
    S.dma('sp', xcT[:], fm(ctxT_in.ap()), writes=['xcT'])
    S.op('dve', lambda e: e.memset(ones1024[:], 1.0 / 1024), writes=['o1'])
    S.op('dve', lambda e: e.memset(ones384[:], 1.0 / 384), writes=['o2'])
    S.op('dve', lambda e: e.memset(ones256[:], 1.0 / 256), writes=['o3'])
    S.op('dve', lambda e: e.memset(onesb[:], 1.0), writes=['o4'])
    S.barrier()

    def act(out, in_, func, reads, writes, **kw):
        return S.op('act', lambda e: e.activation(out, in_, func, **kw), reads, writes)

    def tt(eng, out, a, b, op, reads, writes):
        return S.op(eng, lambda e: e.tensor_tensor(out, a, b, op=op), reads, writes)

    def stt(eng, out, in0, scalar, in1, op0, op1, reads, writes):
        return S.op(eng, lambda e: e.scalar_tensor_tensor(out, in0, scalar, in1, op0=op0, op1=op1), reads, writes)

    def tcopy(eng, out, in_, reads, writes):
        return S.op(eng, lambda e: e.tensor_copy(out, in_), reads, writes)

    def recip(out, in_, reads, writes):
        return S.op('dve', lambda e: e.reciprocal(out, in_), reads, writes)

    def dbg_dump(name, dst_ap, src_ap, reads):
        if name in dbg_t:
            S.dma('sp', dst_ap, src_ap, reads=reads, writes=[('dbg', name, str(uid[0]))])
            uid[0] += 1

    bankc = {'i': 0}

    def nb(lo=0, hi=8):
        b = lo + bankc['i'] % (hi - lo)
        bankc['i'] += 1
        return b

    def rms_rstd(xin, xkeys, nk, n, ones_t, sq, sqk, rstd, rk, bank):
        act(sq[:, 0:nk, :n], xin, AF.Square, reads=xkeys, writes=[sqk])
        S.mm([(PS[:, bank, :n], ones_t[:], sq[:, k, :n], k == 0, k == nk - 1) for k in range(nk)],
             reads=[sqk], writes=[f'ps{bank}'])
        act(rstd[:, :n], PS[:, bank, :n], AF.Ln, reads=[f'ps{bank}'], writes=[rk], bias=EPS)
        act(rstd[:, :n], rstd[:, :n], AF.Exp, reads=[rk], writes=[rk], scale=-0.5)

    def mod_parts(l, es, bank, nring):
        modT, A1, A2 = MODT[l % 2], A1S[l % 2], A2S[l % 2]
        scb = T_(es, [128, 8, 2], BF16)
        wr = [T_(es, [128, 8, 512], BF16) for _ in range(nring)]
        wv_ = fm(w_mod[l])
        tg = f'L{l}'

        def begin():
            act(scb[:], cc[:], AF.Silu, reads=['cc'], writes=['scb' + tg])

        def block(jb):
            wb, wk = wr[jb % nring], f'wmod{tg}{jb % nring}'
            S.dma('pool', wb[:], wv_[:, :, jb * 512:(jb + 1) * 512], writes=[wk])
            for m in range(4):
                c = jb * 4 + m
                S.mm([(PS[:, bank, c * 2:c * 2 + 2], wb[:, k, m * 128:(m + 1) * 128], scb[:, k, :], k == 0, k == 7)
                      for k in range(8)], reads=[wk, 'scb' + tg], writes=[f'psM{tg}{c}'] + ([f'ps{bank}'] if c == 0 else []))

        def end():
            tt('dve', modT[:], PS[:, bank, 0:96].rearrange("p (c j) -> p c j", j=2),
               vecs[:, l, V_BMOD:V_BMOD + 48].unsqueeze(2).to_broadcast([128, 48, 2]), ALU.add,
               reads=[f'psM{tg}{c}' for c in range(48)] + [f'ps{bank}'], writes=['modT' + tg, f'ps{bank}'])
            stt('dve', A1[:], modT[:, 8:16, :], 1.0, vecs[:, l, V_NMIX:V_NMIX + 8].unsqueeze(2).to_broadcast([128, 8, 2]),
                ALU.add, ALU.mult, reads=['modT' + tg], writes=['A1' + tg])
            stt('dve', A2[:], modT[:, 32:40, :], 1.0, vecs[:, l, V_NFFN:V_NFFN + 8].unsqueeze(2).to_broadcast([128, 8, 2]),
                ALU.add, ALU.mult, reads=['modT' + tg], writes=['A2' + tg])
        return begin, block, end

    def phase_mod(l):
        with ExitStack() as es:
            begin, block, end = mod_parts(l, es, 0, 3)
            begin()
            for jb in range(12):
                block(jb)
            end()

    def phase_norm(l, hT):
        modT, A1 = MODT[l % 2], A1S[l % 2]
        xsrc = fm(xT_in.ap()) if l == 0 else xs_v
        with ExitStack() as es:
            xr = [T_(es, [128, 8, 512], F32) for _ in range(2)]
            sq = T_(es, [128, 8, 512], F32)
            rstd = T_(es, [128, 512], F32)
            tmp = [T_(es, [128, 512], F32) for _ in range(2)]
            for ti, (t0, n) in enumerate(TILES):
                j = 0 if ti < 4 else 1
                if ti < 4:
                    xt, xk = xr[ti % 2], f'xr{ti % 2}'
                    S.dma('sp', xt[:, :, :n], xsrc[:, :, t0:t0 + n], writes=[xk])
                    xin = xt[:, :, :n]
                else:
                    xin, xk = xcT[:, :, :n], 'xcT'
                b = nb()
                rms_rstd(xin, [xk], 8, n, ones1024, sq, 'sq', rstd, 'rstd', b)
                for k in range(8):
                    tm, tk = tmp[k % 2], f'tmp{k % 2}'
                    stt('dve', tm[:, :n], xin[:, k, :], A1[:, k, j:j + 1], rstd[:, :n], ALU.mult, ALU.mult,
                        reads=[xk, 'rstd'], writes=[tk])
                    act(hT[:, k, t0:t0 + n], tm[:, :n], AF.Identity, reads=[tk], writes=[('hT', ti, k)],
                        bias=modT[:, k, j:j + 1])
            if 'hT' in dbg_t and l == 0:
                S.dma('sp', fm(dbg_t['hT'].ap()), hT[:], reads=[('hT', ti, k) for ti in range(5) for k in range(8)], writes=['dbg_hT'])

    def rope_apply(out_ap, ps_a, ps_b, rope_t, np_, t0, n, tmpA, tmpB, reads, wkey):
        tt('dve', tmpA[:np_, :n], ps_a, rope_t[:np_, 0, t0:t0 + n], ALU.mult, reads=[reads[0], 'rope'], writes=['ropeA'])
        tt('dve', tmpB[:np_, :n], ps_b, rope_t[:np_, 1, t0:t0 + n], ALU.mult, reads=[reads[1], 'rope'], writes=['ropeB'])
        tt('pool', out_ap, tmpA[:np_, :n], tmpB[:np_, :n], ALU.add, reads=['ropeA', 'ropeB'], writes=[wkey])

    def phase_mla(l, hT, need_ctx, host_mod=None):
        SC = 192.0 ** -0.5
        NBH = 3 if host_mod is not None else 4
        with ExitStack() as es:
            if host_mod is not None:
                mod_begin, mod_block, mod_end = mod_parts(host_mod, es, 3, 2)
                mod_sched = [2, 1, 2, 1, 2, 1, 2, 1]
                mod_next = [0]
            w1 = T_(es, [128, 8, 704], BF16)
            w1s = T_(es, [128, 8, 64], BF16)
            wq = T_(es, [128, 3, 8, 192], BF16)
            wqs = T_(es, [128, 3, 8, 64], BF16)
            wkv = T_(es, [128, 2, 8, 256], BF16)
            rope = T_(es, [128, 2, T], F32)
            cqn = T_(es, [128, 3, TT], BF16)
            ckvn = T_(es, [128, 2, TT], BF16)
            krT = T_(es, [128, TT], BF16)
            sq = T_(es, [128, 3, 512], F32)
            rstd = T_(es, [128, 512], F32)
            tmpA = T_(es, [128, 512], F32)
            tmpB = T_(es, [128, 512], F32)
            knT = [T_(es, [128, TT], BF16) for _ in range(2)]
            vh = [T_(es, [128, 18, 128], BF16) for _ in range(2)]
            qnT = [T_(es, [128, TT], BF16) for _ in range(2)]
            qrT = [T_(es, [128, TT], BF16) for _ in range(2)]
            pts = [T_(es, [128, 512], BF16) for _ in range(4)]
            rden = T_(es, [128, 512], F32)
            ots = [T_(es, [128, 512], BF16) for _ in range(2)]
            S.dma('pool', w1[:], w_in_v[l][:, :, OFF_CQ:OFF_CQ + 704], writes=['w1'])
            S.dma('pool', wq[:], w_q_up[l].rearrange("(k p) (h d) -> p k h d", p=128, d=192), writes=['wq'])
            S.dma('pool', wkv[:], w_kv_up[l].rearrange("(k p) (h d) -> p k h d", p=128, d=256), writes=['wkv'])
            S.dma('sp', rope[:], rope_in.ap(), writes=['rope'])
            for b in range(2):
                for hf in range(2):
                    d0, s0 = b * 32 + hf * 16, b * 32 + (1 - hf) * 16
                    tcopy('pool', w1s[:, :, d0:d0 + 16], w1[:, :, 640 + s0:640 + s0 + 16], reads=['w1'], writes=[f'w1s{b}{hf}'])
                    tcopy('pool', wqs[:, :, :, d0:d0 + 16], wq[:, :, :, 128 + s0:128 + s0 + 16], reads=['wq'], writes=[f'wqs{b}{hf}'])
            w1sk = [f'w1s{b}{hf}' for b in range(2) for hf in range(2)]
            wqsk = [f'wqs{b}{hf}' for b in range(2) for hf in range(2)]
            hk = lambda ti: [('hT', ti, k) for k in range(8)]
            for ti, (t0, n) in enumerate(TILES):
                for c in range(3):
                    S.mm([(PS[:, c, :n], w1[:, k, c * 128:(c + 1) * 128], hT[:, k, t0:t0 + n], k == 0, k == 7) for k in range(8)],
                         reads=['w1'] + hk(ti), writes=[f'ps{c}'])
                rms_rstd(PS[:, 0:3, :n], ['ps0', 'ps1', 'ps2'], 3, n, ones384, sq, 'sq', rstd, 'rstd', 3)
                for c in range(3):
                    stt('dve', cqn[:, c, t0:t0 + n], PS[:, c, :n], vecs[:, l, V_QN + c:V_QN + c + 1], rstd[:, :n], ALU.mult, ALU.mult,
                        reads=[f'ps{c}', 'rstd'], writes=[('cqn', ti, c)])
                for c in range(2):
                    S.mm([(PS[:, 4 + c, :n], w1[:, k, 384 + c * 128:384 + (c + 1) * 128], hT[:, k, t0:t0 + n], k == 0, k == 7) for k in range(8)],
                         reads=['w1'] + hk(ti), writes=[f'ps{4 + c}'])
                rms_rstd(PS[:, 4:6, :n], ['ps4', 'ps5'], 2, n, ones256, sq, 'sq', rstd, 'rstd', 6)
                for c in range(2):
                    stt('dve', ckvn[:, c, t0:t0 + n], PS[:, 4 + c, :n], vecs[:, l, V_KVN + c:V_KVN + c + 1], rstd[:, :n], ALU.mult, ALU.mult,
                        reads=[f'ps{4 + c}', 'rstd'], writes=[('ckvn', ti, c)])
                S.mm([(PS[0:64, 7, :n], w1[:, k, 640:704], hT[:, k, t0:t0 + n], k == 0, k == 7) for k in range(8)],
                     reads=['w1'] + hk(ti), writes=['ps7'])
                if ti < 4:
                    S.mm([(PS[0:64, 0, :n], w1s[:, k, :], hT[:, k, t0:t0 + n], k == 0, k == 7) for k in range(8)],
                         reads=w1sk + hk(ti), writes=['ps0'])
                    rope_apply(krT[0:64, t0:t0 + n], PS[0:64, 7, :n], PS[0:64, 0, :n], rope, 64, t0, n, tmpA, tmpB,
                               ['ps7', 'ps0', 'rope'], ('krT', ti))
                else:
                    act(krT[0:64, t0:t0 + n], PS[0:64, 7, :n], AF.Copy, reads=['ps7'], writes=[('krT', ti)])
            qtiles = list(enumerate(TILES[:4])) + ([(4, TILES[4])] if need_ctx else [])
            if host_mod is not None:
                mod_begin()
            for h in range(8):
                s = h % 2
                if host_mod is not None:
                    for _ in range(mod_sched[h]):
                        mod_block(mod_next[0])
                        mod_next[0] += 1
                for ti, (t0, n) in enumerate(TILES):
                    b = nb(0, NBH)
                    S.mm([(PS[:, b, :n], wkv[:, k, h, 0:128], ckvn[:, k, t0:t0 + n], k == 0, k == 1) for k in range(2)],
                         reads=['wkv', ('ckvn', ti, 0), ('ckvn', ti, 1)], writes=[f'ps{b}'])
                    act(knT[s][:, t0:t0 + n], PS[:, b, :n], AF.Copy, reads=[f'ps{b}'], writes=[('knT', s, ti)])
                for g0 in range(0, 18, 4):
                    ng = min(4, 18 - g0)
                    b = nb(0, NBH)
                    S.mm([(PS[:, b, a * 128:(a + 1) * 128], ckvn[:, k, (g0 + a) * 128:(g0 + a + 1) * 128], wkv[:, k, h, 128:256], k == 0, k == 1)
                          for a in range(ng) for k in range(2)],
                         reads=['wkv'] + [('ckvn', ((g0 + a) * 128) // 512, c) for a in range(ng) for c in range(2)], writes=[f'ps{b}'])
                    tcopy('dve', vh[s][:, g0:g0 + ng, :], PS[:, b, 0:ng * 128].rearrange("p (a d) -> p a d", d=128),
                          reads=[f'ps{b}'], writes=[('vh', s, g0)])
                for ti, (t0, n) in qtiles:
                    b = nb(0, NBH)
                    S.mm([(PS[:, b, :n], wq[:, k, h, 0:128], cqn[:, k, t0:t0 + n], k == 0, k == 2) for k in range(3)],
                         reads=['wq'] + [('cqn', ti, c) for c in range(3)], writes=[f'ps{b}'])
                    act(qnT[s][:, t0:t0 + n], PS[:, b, :n], AF.Copy, reads=[f'ps{b}'], writes=[('qnT', s, ti)])
                    b = nb(0, NBH)
                    S.mm([(PS[0:64, b, :n], wq[:, k, h, 128:192], cqn[:, k, t0:t0 + n], k == 0, k == 2) for k in range(3)],
                         reads=['wq'] + [('cqn', ti, c) for c in range(3)], writes=[f'ps{b}'])
                    if ti < 4:
                        b2 = nb(0, NBH)
                        S.mm([(PS[0:64, b2, :n], wqs[:, k, h, :], cqn[:, k, t0:t0 + n], k == 0, k == 2) for k in range(3)],
                             reads=wqsk + [('cqn', ti, c) for c in range(3)], writes=[f'ps{b2}'])
                        rope_apply(qrT[s][0:64, t0:t0 + n], PS[0:64, b, :n], PS[0:64, b2, :n], rope, 64, t0, n, tmpA, tmpB,
                                   [f'ps{b}', f'ps{b2}', 'rope'], ('qrT', s, ti))
                    else:
                        act(qrT[s][0:64, t0:t0 + n], PS[0:64, b, :n], AF.Copy, reads=[f'ps{b}'], writes=[('qrT', s, ti)])
                steps = []
                for qi, (ti, (q0, qn_)) in enumerate(qtiles):
                    kcs = list(range(18)) if ti < 4 else [16, 17]
                    for idx, kc in enumerate(kcs):
                        steps.append((qi, ti, q0, qn_, kc, idx == 0, idx == len(kcs) - 1))
                LA = 2

                def emit_s(n):
                    qi, ti, q0, qn_, kc, first, last = steps[n]
                    sb = nb(0, NBH)
                    kti = kc // 4
                    S.mm([(PS[:, sb, :qn_], knT[s][:, kc * 128:(kc + 1) * 128], qnT[s][:, q0:q0 + qn_], True, False),
                          (PS[:, sb, :qn_], krT[0:64, kc * 128:(kc + 1) * 128], qrT[s][0:64, q0:q0 + qn_], False, True)],
                         reads=[('knT', s, kti), ('qnT', s, ti), ('krT', kti), ('qrT', s, ti)], writes=[f'ps{sb}'])
                    pt, pk = pts[n % 4], f'pt{n % 4}'
                    act(pt[:, :qn_], PS[:, sb, :qn_], AF.Exp, reads=[f'ps{sb}'], writes=[pk], scale=SC)

                def emit_pv(n):
                    qi, ti, q0, qn_, kc, first, last = steps[n]
                    ob, db = 4 + qi % 2, 6 + qi % 2
                    pt, pk = pts[n % 4], f'pt{n % 4}'
                    items = [(PS[:, ob, :qn_], vh[s][:, kc, :], pt[:, :qn_], first, last),
                             (PS[:, db, :qn_], onesb[:], pt[:, :qn_], first, last)]
                    if first:
                        S.mm(items, reads=[pk, ('vh', s, (kc // 4) * 4)], writes=[f'ps{ob}', f'ps{db}'])
                    else:
                        S.mm(items, reads=[pk, ('vh', s, (kc // 4) * 4)], cont=[f'ps{ob}', f'ps{db}'])
                    if last:
                        recip(rden[:, :qn_], PS[:, db, :qn_], reads=[f'ps{db}'], writes=['rden'])
                        ot, ok = ots[qi % 2], f'ot{qi % 2}'
                        tt('dve', ot[:, :qn_], PS[:, ob, :qn_], rden[:, :qn_], ALU.mult, reads=[f'ps{ob}', 'rden'], writes=[ok])
                        S.dma('sp', yD_v[:, h, q0:q0 + qn_], ot[:, :qn_], reads=[ok], writes=[('yD', h, ti)])

                for n in range(len(steps) + LA):
                    if n < len(steps):
                        emit_s(n)
                    if n - LA >= 0:
                        emit_pv(n - LA)
            if host_mod is not None:
                mod_end()

    def phase_merge(l, hT, br, need_ctx):
        wsrc = [w_pa, w_pb, w_pc][br]
        tiles = TILES if need_ctx else TILES[:4]
        with ExitStack() as es:
            wp = T_(es, [128, 8, D], BF16)
            wg = T_(es, [128, 8, D], BF16)
            yts = [T_(es, [128, 8, 512], BF16) for _ in range(2)]
            mts = [T_(es, [128, 8, 512], F32) for _ in range(2)]
            mos = [T_(es, [128, 8, 512], F32) for _ in range(2)]
            gs = [T_(es, [128, 512], F32) for _ in range(2)]
            tmp = [T_(es, [128, 512], F32) for _ in range(2)]
            wpv = fm(wsrc[l])
            for q in range(4):
                cs = slice(q * 256, (q + 1) * 256)
                S.dma('pool', wp[:, :, cs], wpv[:, :, cs], writes=[('wp', q)])
                S.dma('pool', wg[:, :, cs], w_in_v[l][:, :, OFF_MG + br * D + q * 256:OFF_MG + br * D + (q + 1) * 256], writes=[('wg', q)])

            def load_tile(ti):
                t0, n = tiles[ti]
                s = ti % 2
                for k in range(8):
                    S.dma('sp', yts[s][:, k, :n], yD_v[:, k, t0:t0 + n], writes=[(f'yt{s}', k)])
                if br > 0:
                    S.dma('sp', mts[s][:, :, :n], mD_v[:, :, t0:t0 + n], writes=[f'mt{s}'])

            load_tile(0)
            for ti, (t0, n) in enumerate(tiles):
                s = ti % 2
                if ti + 1 < len(tiles):
                    load_tile(ti + 1)
                for m in range(8):
                    bP, bG = nb(), nb()
                    S.mm([(PS[:, bP, :n], wp[:, k, m * 128:(m + 1) * 128], yts[s][:, k, :n], k == 0, k == 7) for k in range(8)],
                         reads=[('wp', m // 2)] + [(f'yt{s}', k) for k in range(8)], writes=[f'ps{bP}'])
                    S.mm([(PS[:, bG, :n], wg[:, k, m * 128:(m + 1) * 128], hT[:, k, t0:t0 + n], k == 0, k == 7) for k in range(8)],
                         reads=[('wg', m // 2)], writes=[f'ps{bG}'])
                    g, gk = gs[m % 2], f'gs{m % 2}'
                    act(g[:, :n], PS[:, bG, :n], AF.Sigmoid, reads=[f'ps{bG}'], writes=[gk])
                    if br == 0:
                        tt('dve', mos[s][:, m, :n], PS[:, bP, :n], g[:, :n], ALU.mult, reads=[f'ps{bP}', gk], writes=[(f'mo{s}', m)])
                    else:
                        tm, tk = tmp[m % 2], f'tmp{m % 2}'
                        tt('dve', tm[:, :n], PS[:, bP, :n], g[:, :n], ALU.mult, reads=[f'ps{bP}', gk], writes=[tk])
                        tt('pool', mos[s][:, m, :n], tm[:, :n], mts[s][:, m, :n], ALU.add, reads=[tk, f'mt{s}'], writes=[(f'mo{s}', m)])
                S.dma('sp', mD_v[:, :, t0:t0 + n], mos[s][:, :, :n], reads=[(f'mo{s}', m) for m in range(8)], writes=[('mD', ti)])

    def phase_ffn(l, need_ctx, last):
        modT, A2 = MODT[l % 2], A2S[l % 2]
        xsrc = fm(xT_in.ap()) if l == 0 else xs_v
        tiles = TILES if need_ctx else TILES[:4]
        NTL = len(tiles)
        wfi_v = w_fi[l].rearrange("(k p) (g f) -> p k g f", p=128, g=2)
        with ExitStack() as es:
            wo = T_(es, [128, 8, D], BF16)
            wfo = T_(es, [128, 22, D], BF16)
            wfis = [T_(es, [128, 8, 2, 256], BF16) for _ in range(2)]
            mts = [T_(es, [128, 8, 512], BF16) for _ in range(2)]
            x1s = [T_(es, [128, 8, 512], F32) for _ in range(2)]
            h2s = [T_(es, [128, 8, 512], BF16) for _ in range(2)]
            u = T_(es, [128, 22, 512], BF16)
            sq = T_(es, [128, 8, 512], F32)
            rstd = T_(es, [128, 512], F32)
            sgs = [T_(es, [128, 512], F32) for _ in range(2)]
            tmp = [T_(es, [128, 512], F32) for _ in range(2)]
            wov = fm(w_o[l])
            wfo_v = w_fo[l].rearrange("(j p) n -> p j n", p=128)
            wcount = [0]

            def load_ffn_tile(ti):
                t0, n = tiles[ti]
                s = ti % 2
                S.dma('pool', mts[s][:, :, :n], mD_v[:, :, t0:t0 + n], writes=[f'mt{s}'])
                if ti < 4:
                    S.dma('sp', x1s[s][:, :, :n], xsrc[:, :, t0:t0 + n], writes=[(f'x1{s}', m) for m in range(8)])

            def xof(ti):
                return (x1s[ti % 2], f'x1{ti % 2}') if ti < 4 else (xcT, 'xcT')

            def ss_mm(m, n):
                if m == 0:
                    S.mm([(PS[:, 7, :n], ones1024[:], sq[:, m, :n], True, False)], reads=[('sq', m)], writes=['ps7'])
                else:
                    S.mm([(PS[:, 7, :n], ones1024[:], sq[:, m, :n], False, m == 7)], reads=[('sq', m)], cont=['ps7'])

            def rstd_fin(n):
                act(rstd[:, :n], PS[:, 7, :n], AF.Ln, reads=['ps7'], writes=['rstd'], bias=EPS)
                act(rstd[:, :n], rstd[:, :n], AF.Exp, reads=['rstd'], writes=['rstd'], scale=-0.5)

            def stage1(ti):
                t0, n = tiles[ti]
                s = ti % 2
                j = 0 if ti < 4 else 1
                x1, xk = xof(ti)
                h2 = h2s[s]
                for m in range(8):
                    b = nb(0, 7)
                    S.mm([(PS[:, b, :n], wo[:, k, m * 128:(m + 1) * 128], mts[s][:, k, :n], k == 0, k == 7) for k in range(8)],
                         reads=[('wo', m // 2), f'mt{s}'], writes=[f'ps{b}'])
                    stt('dve', x1[:, m, :n], PS[:, b, :n], modT[:, 16 + m, j:j + 1], x1[:, m, :n], ALU.mult, ALU.add,
                        reads=[f'ps{b}', (xk, m)], writes=[(xk, m)])
                    act(sq[:, m, :n], x1[:, m, :n], AF.Square, reads=[(xk, m)], writes=[('sq', m)])
                    if m > 0:
                        ss_mm(m - 1, n)
                ss_mm(7, n)
                rstd_fin(n)
                for k in range(8):
                    tm, tk = tmp[k % 2], f'tmp{k % 2}'
                    stt('dve', tm[:, :n], x1[:, k, :n], A2[:, k, j:j + 1], rstd[:, :n], ALU.mult, ALU.mult,
                        reads=[(xk, k), 'rstd'], writes=[tk])
                    act(h2[:, k, :n], tm[:, :n], AF.Identity, reads=[tk], writes=[(f'h2{s}', k)], bias=modT[:, 24 + k, j:j + 1])

            def stage2(ti):
                t0, n = tiles[ti]
                s = ti % 2
                h2 = h2s[s]
                h2k = [(f'h2{s}', k) for k in range(8)]
                for jb in range(11):
                    ws = wcount[0] % 2
                    wcount[0] += 1
                    wb, wk = wfis[ws], f'wfi{ws}'
                    for gu in range(2):
                        S.dma('pool', wb[:, :, gu, :], wfi_v[:, :, gu, jb * 256:(jb + 1) * 256], writes=[wk + 'gu'[gu]])
                    if ti == 0:
                        for jj in (jb * 2, jb * 2 + 1):
                            S.dma('pool', wfo[:, jj, :], wfo_v[:, jj, :], writes=[('wfo', jj)])
                    for c in range(2):
                        jj = jb * 2 + c
                        bg, bu = nb(0, 7), nb(0, 7)
                        S.mm([(PS[:, bg, :n], wb[:, k, 0, c * 128:(c + 1) * 128], h2[:, k, :n], k == 0, k == 7) for k in range(8)],
                             reads=[wk + 'g'] + h2k, writes=[f'ps{bg}'])
                        S.mm([(PS[:, bu, :n], wb[:, k, 1, c * 128:(c + 1) * 128], h2[:, k, :n], k == 0, k == 7) for k in range(8)],
                             reads=[wk + 'u'] + h2k, writes=[f'ps{bu}'])
                        sg, sk = sgs[jj % 2], f'sg{jj % 2}'
                        act(sg[:, :n], PS[:, bg, :n], AF.Silu, reads=[f'ps{bg}'], writes=[sk])
                        tt('dve', u[:, jj, :n], PS[:, bu, :n], sg[:, :n], ALU.mult, reads=[f'ps{bu}', sk], writes=[('u', jj)])

            def stage3(ti):
                t0, n = tiles[ti]
                j = 0 if ti < 4 else 1
                x1, xk = xof(ti)
                xkeys = [(xk, m) for m in range(8)]
                uk = [('u', jj) for jj in range(22)]
                for m in range(8):
                    b = nb(0, 7)
                    S.mm([(PS[:, b, :n], wfo[:, jj, m * 128:(m + 1) * 128], u[:, jj, :n], jj == 0, jj == 21) for jj in range(22)],
                         reads=[('wfo', jj) for jj in range(22)] + uk, writes=[f'ps{b}'])
                    stt('dve', x1[:, m, :n], PS[:, b, :n], modT[:, 40 + m, j:j + 1], x1[:, m, :n], ALU.mult, ALU.add,
                        reads=[f'ps{b}', (xk, m)], writes=[(xk, m)])
                    if last:
                        act(sq[:, m, :n], x1[:, m, :n], AF.Square, reads=[(xk, m)], writes=[('sq', m)])
                        if m > 0:
                            ss_mm(m - 1, n)
                if not last:
                    if ti < 4:
                        S.dma('sp', xs_v[:, :, t0:t0 + n], x1[:, :, :n], reads=xkeys, writes=[('xs', ti)])
                else:
                    ss_mm(7, n)
                    rstd_fin(n)
                    for k in range(8):
                        stt('dve', sq[:, k, :n], x1[:, k, :n], vecs[:, l, V_FIN + k:V_FIN + k + 1], rstd[:, :n], ALU.mult, ALU.mult,
                            reads=[(xk, k), 'rstd'], writes=[('sq', k)])
                    S.dma('sp', outT_v[:, :, t0:t0 + n], sq[:, :, :n], reads=[('sq', k) for k in range(8)], writes=[('out', ti)])

            S.dma('pool', wo[:, :, 0:256], wov[:, :, 0:256], writes=[('wo', 0)])
            load_ffn_tile(0)
            for q in range(1, 4):
                S.dma('pool', wo[:, :, q * 256:(q + 1) * 256], wov[:, :, q * 256:(q + 1) * 256], writes=[('wo', q)])
            if NTL > 1:
                load_ffn_tile(1)
            stage1(0)
            for ti in range(NTL):
                stage2(ti)
                if ti + 1 < NTL:
                    stage1(ti + 1)
                stage3(ti)
                if ti + 2 < NTL:
                    load_ffn_tile(ti + 2)

    def phase_swa(l, hT, need_ctx):
        with ExitStack() as es:
            wqk = T_(es, [128, 8, 1152], BF16)
            wqks = T_(es, [128, 8, 1152], BF16)
            wv = T_(es, [128, 8, 128], BF16)
            wk2 = T_(es, [128, 8, 128], BF16)
            wk2s = T_(es, [128, 8, 128], BF16)
            rope = T_(es, [128, 2, T], F32)
            trif = T_(es, [128, 2, 128], F32)
            msk = T_(es, [128, 2, 128], BF16)
            esink = T_(es, [128, 8], F32)
            OE = T_(es, [128, 128], BF16)
            OO = T_(es, [128, 128], BF16)
            kT2 = T_(es, [128, TT], BF16)
            VE = T_(es, [128, 18, 128], BF16)
            VO = T_(es, [128, 18, 128], BF16)
            qT = T_(es, [128, 4, TT], BF16)
            tmpA = T_(es, [128, 512], F32)
            tmpB = T_(es, [128, 512], F32)
            pts = [T_(es, [128, 2, 512], BF16) for _ in range(10)]
            rd = T_(es, [128, 4, 128], F32)
            ots = [T_(es, [128, 4, 512], BF16) for _ in range(2)]
            S.dma('pool', wv[:], w_in_v[l][:, :, OFF_VS:OFF_VS + 128], writes=['wv'])
            S.dma('pool', wqk[:, :, 1024:1152], w_in_v[l][:, :, OFF_QS + 1024:OFF_QS + 1152], writes=['wqk_k'])
            for q in range(4):
                S.dma('pool', wqk[:, :, q * 256:(q + 1) * 256], w_in_v[l][:, :, OFF_QS + q * 256:OFF_QS + (q + 1) * 256], writes=[('wqk', q)])
            S.dma('sp', rope[:], rope_in.ap(), writes=['rope'])
            S.dma('sp', trif[:], tri_in[:, 4:6, :], writes=['trif'])
            tcopy('dve', msk[:], trif[:], reads=['trif'], writes=['msk'])
            act(esink[:], vecs[:, l, V_SINK:V_SINK + 8], AF.Exp, reads=[], writes=['esink'])
            S.op('dve', lambda e: e.memset(OE[:], 0.0), writes=['OE'])
            S.op('dve', lambda e: e.memset(OO[:], 0.0), writes=['OO'])
            S.op('dve', lambda e: e.memset(OE[:, 0:64], 1.0), reads=['OE'], writes=['OE'])
            S.op('dve', lambda e: e.memset(OO[:, 64:128], 1.0), reads=['OO'], writes=['OO'])
            S.op('pool', lambda e: e.memset(VE[:], 0.0), writes=['VE'])
            S.op('pool', lambda e: e.memset(VO[:], 0.0), writes=['VO'])
            w4 = wqk[:].rearrange("p k (h d) -> p k h d", d=64)
            w4s = wqks[:].rearrange("p k (h d) -> p k h d", d=64)
            def swap_heads(h0, h1, rkey, wkey):
                ks = []
                for b in range(2):
                    for hf in range(2):
                        d0, s0 = b * 32 + hf * 16, b * 32 + (1 - hf) * 16
                        tcopy('pool', w4s[:, :, h0:h1, d0:d0 + 16], w4[:, :, h0:h1, s0:s0 + 16], reads=[rkey], writes=[(wkey, b, hf)])
                        ks.append((wkey, b, hf))
                return ks
            wsk_k = swap_heads(16, 18, 'wqk_k', 'wqks_k')
            wsk_q = [swap_heads(q * 4, (q + 1) * 4, ('wqk', q), ('wqks', q)) for q in range(4)]
            hk = lambda ti: [('hT', ti, k) for k in range(8)]
            qtiles = list(enumerate(TILES[:4])) + ([(4, TILES[4])] if need_ctx else [])
            import os
            stage = int(os.environ.get("KSTAGE", "9"))
            if stage < 1:
                S.barrier()
                return
            for g in range(2):
                kc0 = 1024 + g * 64
                for hf in range(2):
                    tcopy('pool', wk2[:, :, hf * 64:(hf + 1) * 64], wqk[:, :, kc0:kc0 + 64], reads=['wqk_k'], writes=[f'wk2{hf}'])
                    tcopy('pool', wk2s[:, :, hf * 64:(hf + 1) * 64], wqks[:, :, kc0:kc0 + 64], reads=wsk_k, writes=[f'wk2s{hf}'])
                for g0 in range(0, 18, 8):
                    ng = min(8, 18 - g0)
                    b = nb(0, 4)
                    S.mm([(PS[:, b, a * 64:(a + 1) * 64], hT[:, k, (g0 + a) * 128:(g0 + a + 1) * 128], wv[:, k, g * 64:(g + 1) * 64], k == 0, k == 7)
                          for a in range(ng) for k in range(8)], reads=['wv'], writes=[f'ps{b}'])
                    pv = PS[:, b, 0:ng * 64].rearrange("p (a d) -> p a d", d=64)
                    tcopy('dve', VE[:, g0:g0 + ng, 0:64], pv, reads=[f'ps{b}', 'VE'], writes=[('VE', g0)])
                    tcopy('dve', VO[:, g0:g0 + ng, 64:128], pv, reads=[f'ps{b}', 'VO'], writes=[('VO', g0)])
                for ti, (t0, n) in enumerate(TILES):
                    b = nb(0, 4)
                    S.mm([(PS[:, b, :n], wk2[:, k, :], hT[:, k, t0:t0 + n], k == 0, k == 7) for k in range(8)],
                         reads=['wk20', 'wk21'] + hk(ti), writes=[f'ps{b}'])
                    if ti < 4:
                        b2 = nb(0, 4)
                        S.mm([(PS[:, b2, :n], wk2s[:, k, :], hT[:, k, t0:t0 + n], k == 0, k == 7) for k in range(8)],
                             reads=['wk2s0', 'wk2s1'] + hk(ti), writes=[f'ps{b2}'])
                        rope_apply(kT2[:, t0:t0 + n], PS[:, b, :n], PS[:, b2, :n], rope, 128, t0, n, tmpA, tmpB,
                                   [f'ps{b}', f'ps{b2}', 'rope'], ('kT2', ti))
                    else:
                        act(kT2[:, t0:t0 + n], PS[:, b, :n], AF.Copy, reads=[f'ps{b}'], writes=[('kT2', ti)])
                for pp in range(4):
                    p = g * 4 + pp
                    for ti, (t0, n) in qtiles:
                        b = nb(0, 4)
                        S.mm([(PS[:, b, :n], wqk[:, k, p * 128:(p + 1) * 128], hT[:, k, t0:t0 + n], k == 0, k == 7) for k in range(8)],
                             reads=[('wqk', p // 2)] + hk(ti), writes=[f'ps{b}'])
                        if ti < 4:
                            b2 = nb(0, 4)
                            S.mm([(PS[:, b2, :n], wqks[:, k, p * 128:(p + 1) * 128], hT[:, k, t0:t0 + n], k == 0, k == 7) for k in range(8)],
                                 reads=wsk_q[p // 2] + hk(ti), writes=[f'ps{b2}'])
                            rope_apply(qT[:, pp, t0:t0 + n], PS[:, b, :n], PS[:, b2, :n], rope, 128, t0, n, tmpA, tmpB,
                                       [f'ps{b}', f'ps{b2}', 'rope'], ('qT', pp, ti))
                        else:
                            act(qT[:, pp, t0:t0 + n], PS[:, b, :n], AF.Copy, reads=[f'ps{b}'], writes=[('qT', pp, ti)])
                if stage < 2:
                    continue
                nblk = 18 if need_ctx else 16

                def blk_chunks(i):
                    if i < 16:
                        ch = [(jj, (0 if jj == i - 1 else (1 if jj == i + 1 else None))) for jj in (i - 1, i, i + 1) if 0 <= jj < 16]
                        return ch + [(16, None), (17, None)]
                    return [(16, None), (17, None)]

                def score_chunk(i, ci):
                    q0 = i * 128
                    qk = [('qT', pp, q0 // 512) for pp in range(4)]
                    jj, mk = blk_chunks(i)[ci]
                    bA = 2 * (bankc['i'] % 2)
                    bankc['i'] += 1
                    kk = ('kT2', jj // 4)
                    S.mm([(PS[:, bA, :], kT2[0:64, jj * 128:(jj + 1) * 128], qT[0:64, :, q0:q0 + 128], True, True)],
                         reads=[kk] + qk, writes=[f'ps{bA}'])
                    S.mm([(PS[:, bA + 1, :], kT2[64:128, jj * 128:(jj + 1) * 128], qT[64:128, :, q0:q0 + 128], True, True)],
                         reads=[kk] + qk, writes=[f'ps{bA + 1}'])
                    sl = (i % 2) * 5 + ci
                    pt, pk = pts[sl], f'pt{sl}'
                    act(pt[:], PS[:, bA:bA + 2, :], AF.Exp, reads=[f'ps{bA}', f'ps{bA + 1}'], writes=[pk], scale=0.125)
                    if mk is not None:
                        pv = pt[:].rearrange("p e (a q) -> p (e a) q", q=128)
                        tt('pool', pv, pv, msk[:, mk, :].unsqueeze(1).to_broadcast([128, 8, 128]), ALU.mult,
                           reads=[pk, 'msk'], writes=[pk])

                def pv_pair(i, pp):
                    chunks = blk_chunks(i)
                    nch = len(chunks)
                    par = i % 2
                    ob, db = 4 + par, 6 + par
                    pks = [f'pt{(i % 2) * 5 + ci}' for ci in range(nch)]
                    vk = sorted(set([('VE', (jj // 8) * 8) for jj, _ in chunks] + [('VO', (jj // 8) * 8) for jj, _ in chunks]), key=str) + ['OE', 'OO']
                    cs = slice(pp * 128, (pp + 1) * 128)
                    for bank, (LE, LO) in ((ob, (None, None)), (db, (OE, OO))):
                        items = []
                        for ci, (jj, mk) in enumerate(chunks):
                            pt = pts[(i % 2) * 5 + ci]
                            le = VE[:, jj, :] if LE is None else LE[:]
                            lo = VO[:, jj, :] if LO is None else LO[:]
                            items.append((PS[:, bank, cs], le, pt[:, 0, cs], ci == 0, False))
                            items.append((PS[:, bank, cs], lo, pt[:, 1, cs], False, ci == nch - 1))
                        if pp == 0:
                            S.mm(items, reads=pks + vk, writes=[f'ps{bank}'])
                        else:
                            S.mm(items, reads=pks + vk, cont=[f'ps{bank}'])

                def pv_finish(i):
                    par = i % 2
                    ob, db = 4 + par, 6 + par
                    tt('dve', rd[:], PS[:, db, :].rearrange("p (a q) -> p a q", q=128),
                       esink[:, g * 4:(g + 1) * 4].unsqueeze(2).to_broadcast([128, 4, 128]), ALU.add,
                       reads=[f'ps{db}', 'esink'], writes=['rd'])
                    recip(rd[:], rd[:], reads=['rd'], writes=['rd'])
                    og = (i // 4) % 2
                    ot = ots[og]
                    oc = (i % 4) * 128
                    tt('dve', ot[:, :, oc:oc + 128], PS[:, ob, :].rearrange("p (a q) -> p a q", q=128), rd[:], ALU.mult,
                       reads=[f'ps{ob}', 'rd'], writes=[(f'ot{og}', i % 4)])
                    if i % 4 == 3 or i == nblk - 1:
                        nq = (i % 4 + 1) * 128
                        qg0 = (i // 4) * 512
                        for pp in range(4):
                            S.dma('sp', yD_v[:, g * 4 + pp, qg0:qg0 + nq], ot[:, pp, 0:nq],
                                  reads=[(f'ot{og}', a) for a in range(i % 4 + 1)], writes=[('yD', g, pp, i)])

                def emit_block(i, prev):
                    pairs_left = list(range(4)) if prev is not None else []
                    nci = len(blk_chunks(i)) if i is not None else 0
                    for ci in range(nci):
                        score_chunk(i, ci)
                        if pairs_left:
                            pv_pair(prev, pairs_left.pop(0))
                    while pairs_left:
                        pv_pair(prev, pairs_left.pop(0))
                    if prev is not None:
                        pv_finish(prev)

                for i in range(nblk):
                    emit_block(i, i - 1 if i > 0 else None)
                emit_block(None, nblk - 1)

    def phase_gla(l, hT, need_ctx):
        gw = [w_gkf, w_gkb]
        gb = [b_gkf, b_gkb]
        with ExitStack() as es:
            wg = T_(es, [128, 8, 32], BF16)
            gaug = T_(es, [32, 2, TT], BF16)
            wgk = T_(es, [32, 2, 512], BF16)
            tri = T_(es, [128, 4, 128], F32)
            mk = T_(es, [128, 2, 128], BF16)
            S.dma('pool', wg[:], w_in_v[l][:, :, OFF_GKF:OFF_GKF + 32], writes=['wg'])
            S.dma('sp', tri[:], tri_in[:, 0:4, :], writes=['tri'])
            tcopy('dve', mk[:], tri[:, 0:2, :], reads=['tri'], writes=['mk'])
            trib = T_(es, [128, 4, 128], BF16)
            ones256b = T_(es, [128, 128], BF16)
            tcopy('dve', trib[:], tri[:], reads=['tri'], writes=['trib'])
            S.op('dve', lambda e: e.memset(ones256b[:], 1.0 / 256), writes=['o256b'])
            S.op('dve', lambda e: e.memset(gaug[:], 1.0), writes=['gaug'])
            for d in range(2):
                S.dma('pool', wgk[0:16, d, :], gw[d][l], writes=[f'wgk{d}'])
                S.dma('pool', wgk[16:17, d, :], gb[d][l].unsqueeze(0), writes=[f'wgkb{d}'])
            hk = lambda ti: [('hT', ti, k) for k in range(8)]
            for ti, (t0, n) in enumerate(TILES):
                for d in range(2):
                    b = nb(0, 4)
                    S.mm([(PS[0:16, b, :n], wg[:, k, d * 16:(d + 1) * 16], hT[:, k, t0:t0 + n], k == 0, k == 7) for k in range(8)],
                         reads=['wg'] + hk(ti), writes=[f'ps{b}'])
                    act(gaug[0:16, d, t0:t0 + n], PS[0:16, b, :n], AF.Copy, reads=[f'ps{b}', 'gaug'], writes=[('gaug', d, ti)])
            import os
            gstage = int(os.environ.get("GSTAGE", "9"))
            for grp in range(2):
                if gstage < 1:
                    continue
                with ExitStack() as es2:
                    wq2 = T_(es2, [128, 8, 256], BF16)
                    wk2 = T_(es2, [128, 8, 256], BF16)
                    wv2 = T_(es2, [128, 8, 512], BF16)
                    wga = T_(es2, [128, 8, 512], BF16)
                    qT = T_(es2, [128, 2, TT], BF16)
                    kT = T_(es2, [128, 2, TT], BF16)
                    ktm = T_(es2, [128, 18, 256], BF16)
                    vtm = T_(es2, [128, 18, 512], BF16)
                    obw = T_(es2, [128, 4, TT], BF16)
                    st = [T_(es2, [128, 256], F32) for _ in range(2)]
                    sbf = [T_(es2, [128, 256], BF16) for _ in range(2)]
                    tst = [T_(es2, [128, 256], F32) for _ in range(2)]
                    exs = [T_(es2, [128, 256], F32) for _ in range(2)]
                    nls = [T_(es2, [128, 256], F32) for _ in range(2)]
                    E13s = [T_(es2, [128, 512], F32) for _ in range(3)]
                    E2s = [T_(es2, [128, 2, 128], F32) for _ in range(3)]
                    nhs = [T_(es2, [128, 256], BF16) for _ in range(2)]
                    nlos = [T_(es2, [128, 256], BF16) for _ in range(2)]
                    qds = [T_(es2, [128, 2, 128], BF16) for _ in range(3)]
                    kis = [T_(es2, [128, 2, 128], BF16) for _ in range(3)]
                    kes = [T_(es2, [128, 256], BF16) for _ in range(3)]
                    sTs = [T_(es2, [128, 2, 128], BF16) for _ in range(3)]
                    osum = T_(es2, [128, 4, 128], F32)
                    osq = T_(es2, [128, 4, 128], BF16)
                    rs = T_(es2, [128, 2, 128], F32)
                    sga = T_(es2, [128, 4, TT], BF16)
                    yts = [T_(es2, [128, 4, 512], BF16) for _ in range(2)]
                    g2 = grp * 2
                    S.dma('pool', wq2[:], w_in_v[l][:, :, OFF_QA + g2 * 128:OFF_QA + g2 * 128 + 256], writes=['wq2'])
                    S.dma('pool', wk2[:], w_in_v[l][:, :, OFF_KA + g2 * 128:OFF_KA + g2 * 128 + 256], writes=['wk2'])
                    S.dma('pool', wv2[:], w_in_v[l][:, :, OFF_VA + g2 * 256:OFF_VA + g2 * 256 + 512], writes=['wv2'])
                    S.dma('pool', wga[:], w_in_v[l][:, :, OFF_GA + g2 * 256:OFF_GA + g2 * 256 + 512], writes=['wga'])
                    for ti, (t0, n) in enumerate(TILES):
                        for hh in range(2):
                            b = nb(0, 4)
                            S.mm([(PS[:, b, :n], wq2[:, k, hh * 128:(hh + 1) * 128], hT[:, k, t0:t0 + n], k == 0, k == 7) for k in range(8)],
                                 reads=['wq2'] + hk(ti), writes=[f'ps{b}'])
                            act(qT[:, hh, t0:t0 + n], PS[:, b, :n], AF.Copy, reads=[f'ps{b}'], writes=[('qT', hh, ti)], scale=128.0 ** -0.5)
                            b = nb(0, 4)
                            S.mm([(PS[:, b, :n], wk2[:, k, hh * 128:(hh + 1) * 128], hT[:, k, t0:t0 + n], k == 0, k == 7) for k in range(8)],
                                 reads=['wk2'] + hk(ti), writes=[f'ps{b}'])
                            tcopy('dve', kT[:, hh, t0:t0 + n], PS[:, b, :n], reads=[f'ps{b}'], writes=[('kT', hh, ti)])
                    for t8 in range(18):
                        t128 = t8 * 128
                        b = nb(0, 4)
                        S.mm([(PS[:, b, 0:256], hT[:, k, t128:t128 + 128], wk2[:, k, :], k == 0, k == 7) for k in range(8)],
                             reads=['wk2'] + hk(t128 // 512), writes=[f'ps{b}'])
                        act(ktm[:, t8, :], PS[:, b, 0:256], AF.Copy, reads=[f'ps{b}'], writes=[('ktm', t8)])
                        b = nb(0, 4)
                        S.mm([(PS[:, b, :], hT[:, k, t128:t128 + 128], wv2[:, k, :], k == 0, k == 7) for k in range(8)],
                             reads=['wv2'] + hk(t128 // 512), writes=[f'ps{b}'])
                        tcopy('dve', vtm[:, t8, :], PS[:, b, :], reads=[f'ps{b}'], writes=[('vtm', t8)])
                    for ti, (t0, n) in enumerate(TILES if need_ctx else TILES[:4]):
                        for a in range(4):
                            b = nb(0, 4)
                            S.mm([(PS[:, b, :n], wga[:, k, a * 128:(a + 1) * 128], hT[:, k, t0:t0 + n], k == 0, k == 7) for k in range(8)],
                                 reads=['wga'] + hk(ti), writes=[f'ps{b}'])
                            act(sga[:, a, t0:t0 + n], PS[:, b, :n], AF.Silu, reads=[f'ps{b}'], writes=[('sga', a, ti)])
                    for d in (1, 0):
                        order = [16, 17] + list(range(16)) if d == 0 else [17, 16] + list(range(15, -1, -1))
                        corder = (0, 1) if d == 0 else (1, 0)
                        for hh in range(2):
                            S.op('dve', lambda e, hh=hh: e.memset(st[hh][:], 0.0), reads=[('st', hh)], writes=[('st', hh)])
                            S.op('dve', lambda e, hh=hh: e.memset(sbf[hh][:], 0.0), reads=[('sbf', hh)], writes=[('sbf', hh)])

                        def preA(t8, sl, d=d):
                            tsl = slice(t8 * 128, t8 * 128 + 128)
                            ti = (t8 * 128) // 512
                            K = lambda nm: f'{nm}A{sl}'
                            S.mm([(PS[:, 0, 0:256], gaug[0:17, d, tsl], wgk[0:17, d, grp * 256:(grp + 1) * 256], True, True)],
                                 reads=[('gaug', d, ti), 'gaug', f'wgk{d}', f'wgkb{d}'], writes=['ps0'])
                            act(exs[sl][:], PS[:, 0, 0:256], AF.Exp, reads=['ps0'], writes=[K('ex')], scale=-1.0)
                            act(nls[sl][:], exs[sl][:], AF.Ln, reads=[K('ex')], writes=[K('nl')], bias=1.0)
                            tcopy('pool', nhs[sl][:], nls[sl][:], reads=[K('nl')], writes=[K('nh')])
                            tt('pool', nlos[sl][:], nls[sl][:], nhs[sl][:], ALU.subtract, reads=[K('nl'), K('nh')], writes=[K('nlo')])

                        def preB(t8, sl, sa, d=d):
                            tsl = slice(t8 * 128, t8 * 128 + 128)
                            ti = (t8 * 128) // 512
                            nh, nlo, E13, E2, qd, ki, ke = nhs[sa], nlos[sa], E13s[sl], E2s[sl], qds[sl], kis[sl], kes[sl]
                            K = lambda nm: (f'{nm}A{sa}' if nm in ('nh', 'nlo') else f'{nm}{sl}')
                            items = []
                            for hh in range(2):
                                hs = slice(hh * 128, (hh + 1) * 128)
                                items.append((PS[:, 1, hs], nh[:, hs], trib[:, 2 * d, :], True, False))
                                items.append((PS[:, 1, hs], nlo[:, hs], trib[:, 2 * d, :], False, True))
                            items.append((PS[:, 1, 256:512], trib[:, 2 * d + 1, :], nh[:], True, False))
                            items.append((PS[:, 1, 256:512], trib[:, 2 * d + 1, :], nlo[:], False, True))
                            S.mm(items, reads=[K('nh'), K('nlo'), 'trib'], writes=['ps1'])
                            act(E13[:], PS[:, 1, :], AF.Exp, reads=['ps1'], writes=[K('E1')], scale=-1.0 / 16)
                            act(E2[:], PS[:, 1, 0:256].rearrange("p (h c) -> p h c", c=128), AF.Exp, reads=['ps1'], writes=[K('E2')], scale=1.0 / 16)
                            tt('dve', qd[:], qT[:, :, tsl], E13[:, 0:256].rearrange("p (h c) -> p h c", c=128), ALU.mult,
                               reads=[('qT', 0, ti), ('qT', 1, ti), K('E1')], writes=[K('qd')])
                            tt('pool', ki[:], kT[:, :, tsl], E2[:], ALU.mult, reads=[('kT', 0, ti), ('kT', 1, ti), K('E2')], writes=[K('ki')])
                            tt('pool', ke[:], ktm[:, t8, :], E13[:, 256:512], ALU.mult, reads=[('ktm', t8), K('E1')], writes=[K('ke')])

                        def preC(t8, sl, d=d):
                            K = lambda nm: f'{nm}{sl}'
                            S.mm([(PS[:, 3, hh * 128:(hh + 1) * 128], kis[sl][:, hh, :], qds[sl][:, hh, :], True, True) for hh in range(2)],
                                 reads=[K('ki'), K('qd')], writes=['ps3'])
                            tt('dve', sTs[sl][:], PS[:, 3, 0:256].rearrange("p (h c) -> p h c", c=128),
                               mk[:, d, :].unsqueeze(1).to_broadcast([128, 2, 128]), ALU.mult, reads=['ps3', 'mk'], writes=[K('sT')])

                        def chunk_id(sl, d=d):
                            E13 = E13s[sl]
                            dcol = 127 if d == 0 else 0
                            for hh in range(2):
                                act(tst[hh][:], st[hh][:], AF.Identity, reads=[('st', hh), f'E1{sl}'], writes=[('tst', hh)],
                                    scale=E13[:, hh * 128 + dcol:hh * 128 + dcol + 1])

                        def chunk(t8, sl, ob, nsl, d=d):
                            qd, ke, sT = qds[sl], kes[sl], sTs[sl]
                            K = lambda nm: f'{nm}{sl}'
                            for hh in range(2):
                                items = []
                                for jv in range(2):
                                    oc = (hh * 2 + jv) * 128
                                    vc = slice(hh * 256 + jv * 128, hh * 256 + (jv + 1) * 128)
                                    items.append((PS[:, ob, oc:oc + 128], vtm[:, t8, vc], sT[:, hh, :], True, False))
                                    items.append((PS[:, ob, oc:oc + 128], sbf[hh][:, jv * 128:(jv + 1) * 128], qd[:, hh, :], False, True))
                                S.mm(items, reads=[('vtm', t8), K('sT'), ('sbf', hh), K('qd')], writes=[(f'ps{ob}', hh)])
                                S.mm([(PS[:, 5 + hh, 0:256], ke[:, hh * 128:(hh + 1) * 128], vtm[:, t8, hh * 256:(hh + 1) * 256], True, True)],
                                     reads=[K('ke'), ('vtm', t8)], writes=[('ps5', hh)])
                            for hh in range(2):
                                tt('dve', st[hh][:], PS[:, 5 + hh, 0:256], tst[hh][:], ALU.add,
                                   reads=[('tst', hh), ('ps5', hh)], writes=[('st', hh)])
                            for hh in range(2):
                                act(sbf[hh][:], st[hh][:], AF.Copy, reads=[('st', hh)], writes=[('sbf', hh)])
                            if nsl is not None:
                                chunk_id(nsl)

                        def evac1(t8, ob, d=d):
                            tsl = slice(t8 * 128, t8 * 128 + 128)
                            p4k = [(f'ps{ob}', hh) for hh in range(2)]
                            p4 = PS[:, ob, :].rearrange("p (a q) -> p a q", q=128)
                            if d == 1:
                                act(obw[:, :, tsl], p4, AF.Copy, reads=p4k, writes=[('obw', t8)])
                                return False
                            if not (t8 < 16 or need_ctx):
                                return False
                            tt('dve', osum[:], p4, obw[:, :, tsl], ALU.add, reads=p4k + [('obw', t8)], writes=['osum'])
                            act(osq[:], osum[:], AF.Square, reads=['osum'], writes=['osq'])
                            return True

                        def evac2(t8):
                            tsl = slice(t8 * 128, t8 * 128 + 128)
                            ti = (t8 * 128) // 512
                            S.mm([(PS[:, 0, 256 + hh * 128:256 + (hh + 1) * 128], ones256b[:], osq[:, hh * 2 + jv, :], jv == 0, jv == 1)
                                  for hh in range(2) for jv in range(2)], reads=['osq', 'o256b'], writes=['ps0'])
                            act(rs[:], PS[:, 0, 256:512].rearrange("p (h c) -> p h c", c=128), AF.Ln, reads=['ps0'], writes=['rs'], bias=EPS)
                            act(rs[:], rs[:], AF.Exp, reads=['rs'], writes=['rs'], scale=-0.5)
                            o4 = osum[:].rearrange("p (h j) q -> p h j q", j=2)
                            tt('dve', o4, o4, rs[:].unsqueeze(2).to_broadcast([128, 2, 2, 128]), ALU.mult, reads=['osum', 'rs'], writes=['osum'])
                            yg = (t8 // 4) % 2
                            yt = yts[yg]
                            yc = (t8 % 4) * 128
                            for jv in range(2):
                                stt('dve', yt[:, :, yc:yc + 128].rearrange("p (h j) q -> p j h q", j=2)[:, jv],
                                    osum[:].rearrange("p (h j) q -> p j h q", j=2)[:, jv], vecs[:, l, V_GN + jv:V_GN + jv + 1],
                                    sga[:, :, tsl].rearrange("p (h j) q -> p j h q", j=2)[:, jv], ALU.mult, ALU.mult,
                                    reads=['osum'] + [('sga', a, ti) for a in range(4)], writes=[(f'yt{yg}', t8 % 4, jv), (f'yt{yg}', t8 % 4, jv + 2)])
                            if t8 % 4 == 3 or t8 == 17:
                                nq = (t8 % 4 + 1) * 128
                                tg0 = (t8 // 4) * 512
                                for a in range(4):
                                    S.dma('sp', yD_v[:, grp * 4 + a, tg0:tg0 + nq], yt[:, a, 0:nq],
                                          reads=[(f'yt{yg}', q, a) for q in range(t8 % 4 + 1)], writes=[('yD', grp, a, t8)])

                        NT = len(order)
                        preA(order[0], 0)
                        preA(order[1], 1)
                        preB(order[0], 0, 0)
                        preA(order[2], 0)
                        preB(order[1], 1, 1)
                        preC(order[0], 0)
                        pend = None
                        chunk_id(0)
                        for n, t8 in enumerate(order):
                            ob = 4 if n % 2 == 0 else 7
                            if n + 1 < NT:
                                preC(order[n + 1], (n + 1) % 3)
                            chunk(t8, n % 3, ob, ((n + 1) % 3) if n + 1 < NT else None)
                            if n + 2 < NT:
                                preB(order[n + 2], (n + 2) % 3, (n + 2) % 2)
                            if pend is not None:
                                evac2(pend)
                                pend = None
                            if n + 3 < NT:
                                preA(order[n + 3], (n + 3) % 2)
                            if evac1(t8, ob):
                                pend = t8
                        if pend is not None:
                            evac2(pend)
                    S.barrier()

    import os
    only = os.environ.get("KPHASES", "mod,norm,gla,m0,swa,m1,mla,m2,ffn").split(",")
    n_layers = int(os.environ.get("KLAYERS", n_layers))
    for l in range(n_layers):
        last = l == NL - 1
        need_ctx = not last
        if 'mod' in only and l == 0:
            phase_mod(l)
        S.barrier()
        with ExitStack() as esl:
            hT = T_(esl, [128, 8, TT], BF16, "hT")
            if 'norm' in only:
                phase_norm(l, hT)
            S.barrier()
            if 'gla' in only:
                phase_gla(l, hT, need_ctx)
            S.barrier()
            if 'm0' in only:
                phase_merge(l, hT, 0, need_ctx)
            S.barrier()
            if 'swa' in only:
                phase_swa(l, hT, need_ctx)
            S.barrier()
            if 'm1' in only:
                phase_merge(l, hT, 1, need_ctx)
            S.barrier()
            if 'mla' in only:
                phase_mla(l, hT, need_ctx, host_mod=(l + 1 if l + 1 < n_layers else None))
            S.barrier()
            if 'm2' in only:
                phase_merge(l, hT, 2, need_ctx)
            S.barrier()
        if 'ffn' in only:
            phase_ffn(l, need_ctx, last)
        S.barrier()
    for name in dbg_t:
        if name == 'yD':
            with ExitStack() as es:
                tb = T_(es, [128, 8, TT], BF16)
                S.dma('sp', tb[:], yD_v, writes=['tb'])
                S.dma('sp', fm(dbg_t[name].ap()), tb[:], reads=['tb'], writes=['dbg_yD'])
                S.barrier()
        if name == 'mD':
            with ExitStack() as es:
                tb = T_(es, [128, 8, TT], F32)
                S.dma('sp', tb[:], mD_v, writes=['tb'])
                S.dma('sp', fm(dbg_t[name].ap()), tb[:], reads=['tb'], writes=['dbg_mD'])
                S.barrier()
    S.barrier()
    G.close()
    print(f"[build] instructions={S.n_ins} waits={S.n_wait}")
    return nc


def host_consts():
    t = np.arange(T)
    row, col = t // 64, t % 64
    cos = np.zeros((64, T), np.float32)
    sin = np.zeros((64, T), np.float32)
    for f in range(64):
        blk, i = f // 32, f % 32
        half, idx = i // 16, i % 16
        inv = np.float32(10000.0) ** (-np.float32(idx) / np.float32(16))
        pos = (row if blk == 0 else col).astype(np.float32)
        ang = pos * inv
        cos[f] = np.cos(ang)
        sin[f] = np.sin(ang) * (-1.0 if half == 0 else 1.0)
    rope = np.zeros((128, 2, T), np.float32)
    rope[:64, 0], rope[64:, 0] = cos, cos
    rope[:64, 1], rope[64:, 1] = sin, sin
    a = np.arange(128)
    same = (a[:, None] // 64) == (a[None, :] // 64)
    tri = np.zeros((128, 6, 128), np.float32)
    tri[:, 0] = (a[:, None] <= a[None, :])
    tri[:, 1] = (a[:, None] > a[None, :])
    tri[:, 2] = (a[:, None] >= a[None, :])
    tri[:, 3] = (a[:, None] < a[None, :])
    tri[:, 4] = a[None, :] <= a[:, None]
    tri[:, 5] = a[:, None] <= a[None, :]
    return rope, tri


def pack_vecs(inp):
    v = np.zeros((128, NL, NV), np.float32)
    fmv = lambda z: np.ascontiguousarray(np.asarray(z, np.float32).reshape(-1, 128).T)
    for l in range(NL):
        v[:, l, V_BMOD:V_BMOD + 48] = fmv(inp['b_mod'][l])
        v[:, l, V_NMIX:V_NMIX + 8] = fmv(inp['norm_mix'][l])
        v[:, l, V_NFFN:V_NFFN + 8] = fmv(inp['norm_ffn'][l])
        v[:, l, V_FIN:V_FIN + 8] = fmv(inp['final_norm'])
        v[:, l, V_QN:V_QN + 3] = fmv(inp['q_norm'][l])
        v[:, l, V_KVN:V_KVN + 2] = fmv(inp['kv_norm'][l])
        v[:, l, V_GN:V_GN + 2] = fmv(inp['gla_norm'][l])
        sk = np.asarray(inp['sinks'][l], np.float32)
        for j in range(8):
            v[:64, l, V_SINK + j] = sk[2 * j]
            v[64:, l, V_SINK + j] = sk[2 * j + 1]
    return v


WNAMES = ['w_mod', 'w_in', 'w_gk_fwd', 'b_gk_fwd', 'w_gk_bwd', 'b_gk_bwd', 'w_q_up', 'w_kv_up',
          'w_pa', 'w_pb', 'w_pc', 'w_o', 'w_ffn_in', 'w_ffn_out']


def make_in_maps(inp, cores):
    rope, tri = host_consts()
    vecs = pack_vecs(inp)
    shared = {n: np.ascontiguousarray(np.asarray(inp[n], np.float32)) for n in WNAMES}
    shared.update(rope=rope, tri=tri, vecs=vecs)
    x = np.asarray(inp['x'], np.float32)
    ctx = np.asarray(inp['ctx'], np.float32)
    c = np.asarray(inp['c'], np.float32)
    c_ctx = np.asarray(inp['c_ctx'], np.float32)
    maps = []
    for b in cores:
        m = dict(shared)
        m['xT'] = np.ascontiguousarray(x[b].T)
        m['ctxT'] = np.ascontiguousarray(ctx[b].T)
        cc = np.stack([c[b].reshape(8, 128).T, c_ctx.reshape(8, 128).T], axis=-1)
        m['cc'] = np.ascontiguousarray(cc.astype(np.float32))
        maps.append(m)
    return maps


def kernel(**inputs):
    nc = build()
    maps = make_in_maps(inputs, list(range(8)))
    res = run_bass_kernel_spmd(nc, maps, core_ids=list(range(8)))
    out = np.stack([np.ascontiguousarray(r["outT"].T) for r in res.results], axis=0)
    return out.astype(np.float32)
```

```python
import numpy as np
from contextlib import ExitStack
import concourse.bass as bass
import concourse.mybir as mybir
from concourse.bass_utils import run_bass_kernel_spmd

F32 = mybir.dt.float32
BF16 = mybir.dt.bfloat16
AF = mybir.ActivationFunctionType
ALU = mybir.AluOpType

NL = 2
D = 1024
T = 2048
LC = 256
TT = T + LC
EPS = 1e-6
D_FF = 2816
OFF_QA, OFF_KA, OFF_VA, OFF_GA, OFF_GKF, OFF_GKB = 0, 512, 1024, 2048, 3072, 3088
OFF_QS, OFF_KS, OFF_VS = 3104, 4128, 4256
OFF_CQ, OFF_CKV, OFF_KR, OFF_MG = 4384, 4768, 5024, 5088
D_IN = 8160
TILES = [(0, 512), (512, 512), (1024, 512), (1536, 512), (2048, 256)]
V_BMOD, V_NMIX, V_NFFN, V_FIN, V_QN, V_KVN, V_GN, V_SINK, NV = 0, 48, 56, 64, 72, 75, 77, 79, 87


class Sync:
    ROT = 30000

    def __init__(self, nc, n_dma_sems=24):
        self.nc = nc
        self.eng = {'pe': nc.tensor, 'act': nc.scalar, 'dve': nc.vector, 'pool': nc.gpsimd, 'sp': nc.sync}
        self.sem, self.cnt, self.gen = {}, {}, {}
        for e in ('pe', 'act', 'dve', 'pool'):
            self.gen[e] = 0
            self.sem[e] = nc.alloc_semaphore(f"s_{e}_0")
            self.cnt[e] = 0
        self.dsem = [nc.alloc_semaphore(f"s_dma_{i}") for i in range(n_dma_sems)]
        self.dcnt = [0] * n_dma_sems
        self.dnext = 0
        self.seen = {e: {} for e in self.eng}
        self.res = {}
        self.n_wait = 0
        self.n_ins = 0

    def _need(self, e, ticket, out):
        if ticket is None:
            return
        sem, val = ticket
        k = sem.num
        if e == 'pe' and k == self.sem['pe'].num:
            return
        if self.seen[e].get(k, 0) >= val:
            return
        if k not in out or out[k][1] < val:
            out[k] = (sem, val)

    def _emit_waits(self, e, need):
        for k, (sem, val) in need.items():
            self.eng[e].wait_ge(sem, val)
            self.seen[e][k] = val
            self.n_wait += 1

    def _deps(self, e, reads, writes):
        need = {}
        for r in reads:
            st = self.res.get(r)
            if st is not None:
                self._need(e, st['w'], need)
        for w in writes:
            st = self.res.get(w)
            if st is not None:
                if st['r']:
                    for t in st['r'].values():
                        self._need(e, t, need)
                else:
                    self._need(e, st['w'], need)
        self._emit_waits(e, need)

    def _mark(self, ticket, reads, writes):
        for r in reads:
            st = self.res.setdefault(r, {'w': None, 'r': {}})
            st['r'][ticket[0].num] = ticket
        for w in writes:
            self.res[w] = {'w': ticket, 'r': {}}

    def _signal(self, e, ins):
        if self.cnt[e] >= self.ROT:
            self.gen[e] += 1
            self.sem[e] = self.nc.alloc_semaphore(f"s_{e}_{self.gen[e]}")
            self.cnt[e] = 0
        self.cnt[e] += 1
        ins.then_inc(self.sem[e], 1)
        return (self.sem[e], self.cnt[e])

    def op(self, e, fn, reads=(), writes=()):
        self._deps(e, reads, writes)
        ins = fn(self.eng[e])
        self.n_ins += 1
        t = self._signal(e, ins)
        self._mark(t, reads, writes)
        return t

    def mm(self, items, reads=(), writes=(), cont=()):
        self._deps('pe', reads, writes)
        pe = self.eng['pe']
        ins = None
        for (o, l, r, st, sp) in items:
            ins = pe.matmul(o, l, r, start=st, stop=sp)
            self.n_ins += 1
        t = self._signal('pe', ins)
        self._mark(t, reads, list(writes) + list(cont))
        return t

    def dma(self, q, out, in_, reads=(), writes=()):
        i = self.dnext
        self.dnext = (self.dnext + 1) % len(self.dsem)
        sem = self.dsem[i]
        need = {}
        if self.dcnt[i] > 0:
            self._need(q, (sem, self.dcnt[i]), need)
        self._emit_waits(q, need)
        self._deps(q, reads, writes)
        self.dcnt[i] += 16
        self.eng[q].dma_start(out=out, in_=in_).then_inc(sem, 16)
        self.n_ins += 1
        t = (sem, self.dcnt[i])
        self._mark(t, reads, writes)
        return t

    def barrier(self):
        for e in self.eng:
            need = {}
            for o in ('pe', 'act', 'dve', 'pool'):
                if self.cnt[o] > 0:
                    sem, val = self.sem[o], self.cnt[o]
                    if self.seen[e].get(sem.num, 0) < val:
                        need[sem.num] = (sem, val)
            for i, s in enumerate(self.dsem):
                if self.dcnt[i] > 0 and self.seen[e].get(s.num, 0) < self.dcnt[i]:
                    need[s.num] = (s, self.dcnt[i])
            self._emit_waits(e, need)
        self.res.clear()


def build(n_layers=NL, dbg=None):
    nc = bass.Bass("TRN2", target_bir_lowering=False)

    def din(name, shape, dt=F32):
        return nc.dram_tensor(name, shape, dt, kind="ExternalInput")

    xT_in = din("xT", [D, T])
    ctxT_in = din("ctxT", [D, LC])
    cc_in = din("cc", [128, 8, 2])
    vecs_in = din("vecs", [128, NL, NV])
    rope_in = din("rope", [128, 2, T])
    tri_in = din("tri", [128, 6, 128])
    w_mod = din("w_mod", [NL, D, 6 * D])
    w_in = din("w_in", [NL, D, D_IN])
    w_gkf = din("w_gk_fwd", [NL, 16, 512])
    b_gkf = din("b_gk_fwd", [NL, 512])
    w_gkb = din("w_gk_bwd", [NL, 16, 512])
    b_gkb = din("b_gk_bwd", [NL, 512])
    w_q_up = din("w_q_up", [NL, 384, 1536])
    w_kv_up = din("w_kv_up", [NL, 256, 2048])
    w_pa = din("w_pa", [NL, D, D])
    w_pb = din("w_pb", [NL, D, D])
    w_pc = din("w_pc", [NL, D, D])
    w_o = din("w_o", [NL, D, D])
    w_fi = din("w_ffn_in", [NL, D, 2 * D_FF])
    w_fo = din("w_ffn_out", [NL, D_FF, D])
    outT = nc.dram_tensor("outT", [D, T], F32, kind="ExternalOutput")
    xs = nc.dram_tensor("xs_scr", [D, T], F32)
    mD = nc.dram_tensor("m_scr", [D, TT], F32)
    import os as _os
    yD = nc.dram_tensor("y_scr", [D, TT], BF16, kind=("ExternalOutput" if _os.environ.get("KDUMP") else "Internal"))
    dbg_t = {}
    if dbg:
        for name, (shape, dt) in dbg.items():
            dbg_t[name] = nc.dram_tensor("dbg_" + name, shape, dt, kind="ExternalOutput")

    def fm(ap):
        return ap.rearrange("(k p) t -> p k t", p=128)

    xs_v, mD_v, yD_v, outT_v = fm(xs.ap()), fm(mD.ap()), fm(yD.ap()), fm(outT.ap())
    w_in_v = [fm(w_in[l]) for l in range(NL)]

    S = Sync(nc)
    uid = [0]

    def T_(es, shape, dt, name="t"):
        uid[0] += 1
        return es.enter_context(nc.sbuf_tensor(f"{name}_{uid[0]}", shape, dt))

    PS = nc.alloc_psum_tensor("PS", [128, 8, 512], F32)

    G = ExitStack()
    vecs = T_(G, [128, NL, NV], F32, "vecs")
    cc = T_(G, [128, 8, 2], F32, "cc")
    xcT = T_(G, [128, 8, LC], F32, "xcT")
    MODT = [T_(G, [128, 48, 2], F32, f"modT{i}") for i in range(2)]
    A1S = [T_(G, [128, 8, 2], F32, f"A1{i}") for i in range(2)]
    A2S = [T_(G, [128, 8, 2], F32, f"A2{i}") for i in range(2)]
    ones1024 = T_(G, [128, 128], F32, "o1024")
    ones384 = T_(G, [128, 128], F32, "o384")
    ones256 = T_(G, [128, 128], F32, "o256")
    onesb = T_(G, [128, 128], BF16, "onesb")
    S.dma('sp', vecs[:], vecs_in.ap(), writes=['vecs'])
    S.dma('sp', cc[:], cc_in.ap(), writes=['cc'])
    S.dma('sp', xcT[:], fm(ctxT_in.ap()), writes=['xcT'])
    S.op('dve', lambda e: e.memset(ones1024[:], 1.0 / 1024), writes=['o1'])
    S.op('dve', lambda e: e.memset(ones384[:], 1.0 / 384), writes=['o2'])
    S.op('dve', lambda e: e.memset(ones256[:], 1.0 / 256), writes=['o3'])
    S.op('dve', lambda e: e.memset(onesb[:], 1.0), writes=['o4'])
    S.barrier()

    def act(out, in_, func, reads, writes, **kw):
        return S.op('act', lambda e: e.activation(out, in_, func, **kw), reads, writes)

    def tt(eng, out, a, b, op, reads, writes):
        return S.op(eng, lambda e: e.tensor_tensor(out, a, b, op=op), reads, writes)

    def stt(eng, out, in0, scalar, in1, op0, op1, reads, writes):
        return S.op(eng, lambda e: e.scalar_tensor_tensor(out, in0, scalar, in1, op0=op0, op1=op1), reads, writes)

    def tcopy(eng, out, in_, reads, writes):
        return S.op(eng, lambda e: e.tensor_copy(out, in_), reads, writes)

    def recip(out, in_, reads, writes):
        return S.op('dve', lambda e: e.reciprocal(out, in_), reads, writes)

    def dbg_dump(name, dst_ap, src_ap, reads):
        if name in dbg_t:
            S.dma('sp', dst_ap, src_ap, reads=reads, writes=[('dbg', name, str(uid[0]))])
            uid[0] += 1

    bankc = {'i': 0}

    def nb(lo=0, hi=8):
        b = lo + bankc['i'] % (hi - lo)
        bankc['i'] += 1
        return b

    def rms_rstd(xin, xkeys, nk, n, ones_t, sq, sqk, rstd, rk, bank):
        act(sq[:, 0:nk, :n], xin, AF.Square, reads=xkeys, writes=[sqk])
        S.mm([(PS[:, bank, :n], ones_t[:], sq[:, k, :n], k == 0, k == nk - 1) for k in range(nk)],
             reads=[sqk], writes=[f'ps{bank}'])
        act(rstd[:, :n], PS[:, bank, :n], AF.Ln, reads=[f'ps{bank}'], writes=[rk], bias=EPS)
        act(rstd[:, :n], rstd[:, :n], AF.Exp, reads=[rk], writes=[rk], scale=-0.5)

    def mod_parts(l, es, bank, nring):
        modT, A1, A2 = MODT[l % 2], A1S[l % 2], A2S[l % 2]
        scb = T_(es, [128, 8, 2], BF16)
        wr = [T_(es, [128, 8, 512], BF16) for _ in range(nring)]
        wv_ = fm(w_mod[l])
        tg = f'L{l}'

        def begin():
            act(scb[:], cc[:], AF.Silu, reads=['cc'], writes=['scb' + tg])

        def block(jb):
            wb, wk = wr[jb % nring], f'wmod{tg}{jb % nring}'
            S.dma('pool', wb[:], wv_[:, :, jb * 512:(jb + 1) * 512], writes=[wk])
            for m in range(4):
                c = jb * 4 + m
                S.mm([(PS[:, bank, c * 2:c * 2 + 2], wb[:, k, m * 128:(m + 1) * 128], scb[:, k, :], k == 0, k == 7)
                      for k in range(8)], reads=[wk, 'scb' + tg], writes=[f'psM{tg}{c}'] + ([f'ps{bank}'] if c == 0 else []))

        def end():
            tt('dve', modT[:], PS[:, bank, 0:96].rearrange("p (c j) -> p c j", j=2),
               vecs[:, l, V_BMOD:V_BMOD + 48].unsqueeze(2).to_broadcast([128, 48, 2]), ALU.add,
               reads=[f'psM{tg}{c}' for c in range(48)] + [f'ps{bank}'], writes=['modT' + tg, f'ps{bank}'])
            stt('dve', A1[:], modT[:, 8:16, :], 1.0, vecs[:, l, V_NMIX:V_NMIX + 8].unsqueeze(2).to_broadcast([128, 8, 2]),
                ALU.add, ALU.mult, reads=['modT' + tg], writes=['A1' + tg])
            stt('dve', A2[:], modT[:, 32:40, :], 1.0, vecs[:, l, V_NFFN:V_NFFN + 8].unsqueeze(2).to_broadcast([128, 8, 2]),
                ALU.add, ALU.mult, reads=['modT' + tg], writes=['A2' + tg])
        return begin, block, end

    def phase_mod(l):
        with ExitStack() as es:
            begin, block, end = mod_parts(l, es, 0, 3)
            begin()
            for jb in range(12):
                block(jb)
            end()

    def phase_norm(l, hT):
        modT, A1 = MODT[l % 2], A1S[l % 2]
        xsrc = fm(xT_in.ap()) if l == 0 else xs_v
        with ExitStack() as es:
            xr = [T_(es, [128, 8, 512], F32) for _ in range(2)]
            sq = T_(es, [128, 8, 512], F32)
            rstd = T_(es, [128, 512], F32)
            tmp = [T_(es, [128, 512], F32) for _ in range(2)]
            for ti, (t0, n) in enumerate(TILES):
                j = 0 if ti < 4 else 1
                if ti < 4:
                    xt, xk = xr[ti % 2], f'xr{ti % 2}'
                    S.dma('sp', xt[:, :, :n], xsrc[:, :, t0:t0 + n], writes=[xk])
                    xin = xt[:, :, :n]
                else:
                    xin, xk = xcT[:, :, :n], 'xcT'
                b = nb()
                rms_rstd(xin, [xk], 8, n, ones1024, sq, 'sq', rstd, 'rstd', b)
                for k in range(8):
                    tm, tk = tmp[k % 2], f'tmp{k % 2}'
                    stt('dve', tm[:, :n], xin[:, k, :], A1[:, k, j:j + 1], rstd[:, :n], ALU.mult, ALU.mult,
                        reads=[xk, 'rstd'], writes=[tk])
                    act(hT[:, k, t0:t0 + n], tm[:, :n], AF.Identity, reads=[tk], writes=[('hT', ti, k)],
                        bias=modT[:, k, j:j + 1])
            if 'hT' in dbg_t and l == 0:
                S.dma('sp', fm(dbg_t['hT'].ap()), hT[:], reads=[('hT', ti, k) for ti in range(5) for k in range(8)], writes=['dbg_hT'])

    def rope_apply(out_ap, ps_a, ps_b, rope_t, np_, t0, n, tmpA, tmpB, reads, wkey):
        tt('dve', tmpA[:np_, :n], ps_a, rope_t[:np_, 0, t0:t0 + n], ALU.mult, reads=[reads[0], 'rope'], writes=['ropeA'])
        tt('dve', tmpB[:np_, :n], ps_b, rope_t[:np_, 1, t0:t0 + n], ALU.mult, reads=[reads[1], 'rope'], writes=['ropeB'])
        tt('pool', out_ap, tmpA[:np_, :n], tmpB[:np_, :n], ALU.add, reads=['ropeA', 'ropeB'], writes=[wkey])

    def phase_mla(l, hT, need_ctx, host_mod=None):
        SC = 192.0 ** -0.5
        NBH = 3 if host_mod is not None else 4
        with ExitStack() as es:
            if host_mod is not None:
                mod_begin, mod_block, mod_end = mod_parts(host_mod, es, 3, 2)
                mod_sched = [2, 1, 2, 1, 2, 1, 2, 1]
                mod_next = [0]
            w1 = T_(es, [128, 8, 704], BF16)
            w1s = T_(es, [128, 8, 64], BF16)
            wq = T_(es, [128, 3, 8, 192], BF16)
            wqs = T_(es, [128, 3, 8, 64], BF16)
            wkv = T_(es, [128, 2, 8, 256], BF16)
            rope = T_(es, [128, 2, T], F32)
            cqn = T_(es, [128, 3, TT], BF16)
            ckvn = T_(es, [128, 2, TT], BF16)
            krT = T_(es, [128, TT], BF16)
            sq = T_(es, [128, 3, 512], F32)
            rstd = T_(es, [128, 512], F32)
            tmpA = T_(es, [128, 512], F32)
            tmpB = T_(es, [128, 512], F32)
            knT = [T_(es, [128, TT], BF16) for _ in range(2)]
            vh = [T_(es, [128, 18, 128], BF16) for _ in range(2)]
            qnT = [T_(es, [128, TT], BF16) for _ in range(2)]
            qrT = [T_(es, [128, TT], BF16) for _ in range(2)]
            pts = [T_(es, [128, 512], BF16) for _ in range(4)]
            rden = T_(es, [128, 512], F32)
            ots = [T_(es, [128, 512], BF16) for _ in range(2)]
            S.dma('pool', w1[:], w_in_v[l][:, :, OFF_CQ:OFF_CQ + 704], writes=['w1'])
            S.dma('pool', wq[:], w_q_up[l].rearrange("(k p) (h d) -> p k h d", p=128, d=192), writes=['wq'])
            S.dma('pool', wkv[:], w_kv_up[l].rearrange("(k p) (h d) -> p k h d", p=128, d=256), writes=['wkv'])
            S.dma('sp', rope[:], rope_in.ap(), writes=['rope'])
            for b in range(2):
                for hf in range(2):
                    d0, s0 = b * 32 + hf * 16, b * 32 + (1 - hf) * 16
                    tcopy('pool', w1s[:, :, d0:d0 + 16], w1[:, :, 640 + s0:640 + s0 + 16], reads=['w1'], writes=[f'w1s{b}{hf}'])
                    tcopy('pool', wqs[:, :, :, d0:d0 + 16], wq[:, :, :, 128 + s0:128 + s0 + 16], reads=['wq'], writes=[f'wqs{b}{hf}'])
            w1sk = [f'w1s{b}{hf}' for b in range(2) for hf in range(2)]
            wqsk = [f'wqs{b}{hf}' for b in range(2) for hf in range(2)]
            hk = lambda ti: [('hT', ti, k) for k in range(8)]
            for ti, (t0, n) in enumerate(TILES):
                for c in range(3):
                    S.mm([(PS[:, c, :n], w1[:, k, c * 128:(c + 1) * 128], hT[:, k, t0:t0 + n], k == 0, k == 7) for k in range(8)],
                         reads=['w1'] + hk(ti), writes=[f'ps{c}'])
                rms_rstd(PS[:, 0:3, :n], ['ps0', 'ps1', 'ps2'], 3, n, ones384, sq, 'sq', rstd, 'rstd', 3)
                for c in range(3):
                    stt('dve', cqn[:, c, t0:t0 + n], PS[:, c, :n], vecs[:, l, V_QN + c:V_QN + c + 1], rstd[:, :n], ALU.mult, ALU.mult,
                        reads=[f'ps{c}', 'rstd'], writes=[('cqn', ti, c)])
                for c in range(2):
                    S.mm([(PS[:, 4 + c, :n], w1[:, k, 384 + c * 128:384 + (c + 1) * 128], hT[:, k, t0:t0 + n], k == 0, k == 7) for k in range(8)],
                         reads=['w1'] + hk(ti), writes=[f'ps{4 + c}'])
                rms_rstd(PS[:, 4:6, :n], ['ps4', 'ps5'], 2, n, ones256, sq, 'sq', rstd, 'rstd', 6)
                for c in range(2):
                    stt('dve', ckvn[:, c, t0:t0 + n], PS[:, 4 + c, :n], vecs[:, l, V_KVN + c:V_KVN + c + 1], rstd[:, :n], ALU.mult, ALU.mult,
                        reads=[f'ps{4 + c}', 'rstd'], writes=[('ckvn', ti, c)])
                S.mm([(PS[0:64, 7, :n], w1[:, k, 640:704], hT[:, k, t0:t0 + n], k == 0, k == 7) for k in range(8)],
                     reads=['w1'] + hk(ti), writes=['ps7'])
                if ti < 4:
                    S.mm([(PS[0:64, 0, :n], w1s[:, k, :], hT[:, k, t0:t0 + n], k == 0, k == 7) for k in range(8)],
                         reads=w1sk + hk(ti), writes=['ps0'])
                    rope_apply(krT[0:64, t0:t0 + n], PS[0:64, 7, :n], PS[0:64, 0, :n], rope, 64, t0, n, tmpA, tmpB,
                               ['ps7', 'ps0', 'rope'], ('krT', ti))
                else:
                    act(krT[0:64, t0:t0 + n], PS[0:64, 7, :n], AF.Copy, reads=['ps7'], writes=[('krT', ti)])
            qtiles = list(enumerate(TILES[:4])) + ([(4, TILES[4])] if need_ctx else [])
            if host_mod is not None:
                mod_begin()
            for h in range(8):
                s = h % 2
                if host_mod is not None:
                    for _ in range(mod_sched[h]):
                        mod_block(mod_next[0])
                        mod_next[0] += 1
                for ti, (t0, n) in enumerate(TILES):
                    b = nb(0, NBH)
                    S.mm([(PS[:, b, :n], wkv[:, k, h, 0:128], ckvn[:, k, t0:t0 + n], k == 0, k == 1) for k in range(2)],
                         reads=['wkv', ('ckvn', ti, 0), ('ckvn', ti, 1)], writes=[f'ps{b}'])
                    act(knT[s][:, t0:t0 + n], PS[:, b, :n], AF.Copy, reads=[f'ps{b}'], writes=[('knT', s, ti)])
                for g0 in range(0, 18, 4):
                    ng = min(4, 18 - g0)
                    b = nb(0, NBH)
                    S.mm([(PS[:, b, a * 128:(a + 1) * 128], ckvn[:, k, (g0 + a) * 128:(g0 + a + 1) * 128], wkv[:, k, h, 128:256], k == 0, k == 1)
                          for a in range(ng) for k in range(2)],
                         reads=['wkv'] + [('ckvn', ((g0 + a) * 128) // 512, c) for a in range(ng) for c in range(2)], writes=[f'ps{b}'])
                    tcopy('dve', vh[s][:, g0:g0 + ng, :], PS[:, b, 0:ng * 128].rearrange("p (a d) -> p a d", d=128),
                          reads=[f'ps{b}'], writes=[('vh', s, g0)])
                for ti, (t0, n) in qtiles:
                    b = nb(0, NBH)
                    S.mm([(PS[:, b, :n], wq[:, k, h, 0:128], cqn[:, k, t0:t0 + n], k == 0, k == 2) for k in range(3)],
                         reads=['wq'] + [('cqn', ti, c) for c in range(3)], writes=[f'ps{b}'])
                    act(qnT[s][:, t0:t0 + n], PS[:, b, :n], AF.Copy, reads=[f'ps{b}'], writes=[('qnT', s, ti)])
                    b = nb(0, NBH)
                    S.mm([(PS[0:64, b, :n], wq[:, k, h, 128:192], cqn[:, k, t0:t0 + n], k == 0, k == 2) for k in range(3)],
                         reads=['wq'] + [('cqn', ti, c) for c in range(3)], writes=[f'ps{b}'])
                    if ti < 4:
                        b2 = nb(0, NBH)
                        S.mm([(PS[0:64, b2, :n], wqs[:, k, h, :], cqn[:, k, t0:t0 + n], k == 0, k == 2) for k in range(3)],
                             reads=wqsk + [('cqn', ti, c) for c in range(3)], writes=[f'ps{b2}'])
                        rope_apply(qrT[s][0:64, t0:t0 + n], PS[0:64, b, :n], PS[0:64, b2, :n], rope, 64, t0, n, tmpA, tmpB,
                                   [f'ps{b}', f'ps{b2}', 'rope'], ('qrT', s, ti))
                    else:
                        act(qrT[s][0:64, t0:t0 + n], PS[0:64, b, :n], AF.Copy, reads=[f'ps{b}'], writes=[('qrT', s, ti)])
                steps = []
                for qi, (ti, (q0, qn_)) in enumerate(qtiles):
                    kcs = list(range(18)) if ti < 4 else [16, 17]
                    for idx, kc in enumerate(kcs):
                        steps.append((qi, ti, q0, qn_, kc, idx == 0, idx == len(kcs) - 1))
                LA = 2

                def emit_s(n):
                    qi, ti, q0, qn_, kc, first, last = steps[n]
                    sb = nb(0, NBH)
                    kti = kc // 4
                    S.mm([(PS[:, sb, :qn_], knT[s][:, kc * 128:(kc + 1) * 128], qnT[s][:, q0:q0 + qn_], True, False),
                          (PS[:, sb, :qn_], krT[0:64, kc * 128:(kc + 1) * 128], qrT[s][0:64, q0:q0 + qn_], False, True)],
                         reads=[('knT', s, kti), ('qnT', s, ti), ('krT', kti), ('qrT', s, ti)], writes=[f'ps{sb}'])
                    pt, pk = pts[n % 4], f'pt{n % 4}'
                    act(pt[:, :qn_], PS[:, sb, :qn_], AF.Exp, reads=[f'ps{sb}'], writes=[pk], scale=SC)

                def emit_pv(n):
                    qi, ti, q0, qn_, kc, first, last = steps[n]
                    ob, db = 4 + qi % 2, 6 + qi % 2
                    pt, pk = pts[n % 4], f'pt{n % 4}'
                    items = [(PS[:, ob, :qn_], vh[s][:, kc, :], pt[:, :qn_], first, last),
                             (PS[:, db, :qn_], onesb[:], pt[:, :qn_], first, last)]
                    if first:
                        S.mm(items, reads=[pk, ('vh', s, (kc // 4) * 4)], writes=[f'ps{ob}', f'ps{db}'])
                    else:
                        S.mm(items, reads=[pk, ('vh', s, (kc // 4) * 4)], cont=[f'ps{ob}', f'ps{db}'])
                    if last:
                        recip(rden[:, :qn_], PS[:, db, :qn_], reads=[f'ps{db}'], writes=['rden'])
                        ot, ok = ots[qi % 2], f'ot{qi % 2}'
                        tt('dve', ot[:, :qn_], PS[:, ob, :qn_], rden[:, :qn_], ALU.mult, reads=[f'ps{ob}', 'rden'], writes=[ok])
                        S.dma('sp', yD_v[:, h, q0:q0 + qn_], ot[:, :qn_], reads=[ok], writes=[('yD', h, ti)])

                for n in range(len(steps) + LA):
                    if n < len(steps):
                        emit_s(n)
                    if n - LA >= 0:
                        emit_pv(n - LA)
            if host_mod is not None:
                mod_end()

    def phase_merge(l, hT, br, need_ctx):
        wsrc = [w_pa, w_pb, w_pc][br]
        tiles = TILES if need_ctx else TILES[:4]
        with ExitStack() as es:
            wp = T_(es, [128, 8, D], BF16)
            wg = T_(es, [128, 8, D], BF16)
            yts = [T_(es, [128, 8, 512], BF16) for _ in range(2)]
            mts = [T_(es, [128, 8, 512], F32) for _ in range(2)]
            mos = [T_(es, [128, 8, 512], F32) for _ in range(2)]
            gs = [T_(es, [128, 512], F32) for _ in range(2)]
            tmp = [T_(es, [128, 512], F32) for _ in range(2)]
            wpv = fm(wsrc[l])
            for q in range(4):
                cs = slice(q * 256, (q + 1) * 256)
                S.dma('pool', wp[:, :, cs], wpv[:, :, cs], writes=[('wp', q)])
                S.dma('pool', wg[:, :, cs], w_in_v[l][:, :, OFF_MG + br * D + q * 256:OFF_MG + br * D + (q + 1) * 256], writes=[('wg', q)])

            def load_tile(ti):
                t0, n = tiles[ti]
                s = ti % 2
                for k in range(8):
                    S.dma('sp', yts[s][:, k, :n], yD_v[:, k, t0:t0 + n], writes=[(f'yt{s}', k)])
                if br > 0:
                    S.dma('sp', mts[s][:, :, :n], mD_v[:, :, t0:t0 + n], writes=[f'mt{s}'])

            load_tile(0)
            for ti, (t0, n) in enumerate(tiles):
                s = ti % 2
                if ti + 1 < len(tiles):
                    load_tile(ti + 1)
                for m in range(8):
                    bP, bG = nb(), nb()
                    S.mm([(PS[:, bP, :n], wp[:, k, m * 128:(m + 1) * 128], yts[s][:, k, :n], k == 0, k == 7) for k in range(8)],
                         reads=[('wp', m // 2)] + [(f'yt{s}', k) for k in range(8)], writes=[f'ps{bP}'])
                    S.mm([(PS[:, bG, :n], wg[:, k, m * 128:(m + 1) * 128], hT[:, k, t0:t0 + n], k == 0, k == 7) for k in range(8)],
                         reads=[('wg', m // 2)], writes=[f'ps{bG}'])
                    g, gk = gs[m % 2], f'gs{m % 2}'
                    act(g[:, :n], PS[:, bG, :n], AF.Sigmoid, reads=[f'ps{bG}'], writes=[gk])
                    if br == 0:
                        tt('dve', mos[s][:, m, :n], PS[:, bP, :n], g[:, :n], ALU.mult, reads=[f'ps{bP}', gk], writes=[(f'mo{s}', m)])
                    else:
                        tm, tk = tmp[m % 2], f'tmp{m % 2}'
                        tt('dve', tm[:, :n], PS[:, bP, :n], g[:, :n], ALU.mult, reads=[f'ps{bP}', gk], writes=[tk])
                        tt('pool', mos[s][:, m, :n], tm[:, :n], mts[s][:, m, :n], ALU.add, reads=[tk, f'mt{s}'], writes=[(f'mo{s}', m)])
                S.dma('sp', mD_v[:, :, t0:t0 + n], mos[s][:, :, :n], reads=[(f'mo{s}', m) for m in range(8)], writes=[('mD', ti)])

    def phase_ffn(l, need_ctx, last):
        modT, A2 = MODT[l % 2], A2S[l % 2]
        xsrc = fm(xT_in.ap()) if l == 0 else xs_v
        tiles = TILES if need_ctx else TILES[:4]
        NTL = len(tiles)
        wfi_v = w_fi[l].rearrange("(k p) (g f) -> p k g f", p=128, g=2)
        with ExitStack() as es:
            wo = T_(es, [128, 8, D], BF16)
            wfo = T_(es, [128, 22, D], BF16)
            wfis = [T_(es, [128, 8, 2, 256], BF16) for _ in range(2)]
            mts = [T_(es, [128, 8, 512], BF16) for _ in range(2)]
            x1s = [T_(es, [128, 8, 512], F32) for _ in range(2)]
            h2s = [T_(es, [128, 8, 512], BF16) for _ in range(2)]
            u = T_(es, [128, 22, 512], BF16)
            sq = T_(es, [128, 8, 512], F32)
            rstd = T_(es, [128, 512], F32)
            sgs = [T_(es, [128, 512], F32) for _ in range(2)]
            tmp = [T_(es, [128, 512], F32) for _ in range(2)]
            wov = fm(w_o[l])
            wfo_v = w_fo[l].rearrange("(j p) n -> p j n", p=128)
            wcount = [0]

            def load_ffn_tile(ti):
                t0, n = tiles[ti]
                s = ti % 2
                S.dma('pool', mts[s][:, :, :n], mD_v[:, :, t0:t0 + n], writes=[f'mt{s}'])
                if ti < 4:
                    S.dma('sp', x1s[s][:, :, :n], xsrc[:, :, t0:t0 + n], writes=[(f'x1{s}', m) for m in range(8)])

            def xof(ti):
                return (x1s[ti % 2], f'x1{ti % 2}') if ti < 4 else (xcT, 'xcT')

            def ss_mm(m, n):
                if m == 0:
                    S.mm([(PS[:, 7, :n], ones1024[:], sq[:, m, :n], True, False)], reads=[('sq', m)], writes=['ps7'])
                else:
                    S.mm([(PS[:, 7, :n], ones1024[:], sq[:, m, :n], False, m == 7)], reads=[('sq', m)], cont=['ps7'])

            def rstd_fin(n):
                act(rstd[:, :n], PS[:, 7, :n], AF.Ln, reads=['ps7'], writes=['rstd'], bias=EPS)
                act(rstd[:, :n], rstd[:, :n], AF.Exp, reads=['rstd'], writes=['rstd'], scale=-0.5)

            def stage1(ti):
                t0, n = tiles[ti]
                s = ti % 2
                j = 0 if ti < 4 else 1
                x1, xk = xof(ti)
                h2 = h2s[s]
                for m in range(8):
                    b = nb(0, 7)
                    S.mm([(PS[:, b, :n], wo[:, k, m * 128:(m + 1) * 128], mts[s][:, k, :n], k == 0, k == 7) for k in range(8)],
                         reads=[('wo', m // 2), f'mt{s}'], writes=[f'ps{b}'])
                    stt('dve', x1[:, m, :n], PS[:, b, :n], modT[:, 16 + m, j:j + 1], x1[:, m, :n], ALU.mult, ALU.add,
                        reads=[f'ps{b}', (xk, m)], writes=[(xk, m)])
                    act(sq[:, m, :n], x1[:, m, :n], AF.Square, reads=[(xk, m)], writes=[('sq', m)])
                    if m > 0:
                        ss_mm(m - 1, n)
                ss_mm(7, n)
                rstd_fin(n)
                for k in range(8):
                    tm, tk = tmp[k % 2], f'tmp{k % 2}'
                    stt('dve', tm[:, :n], x1[:, k, :n], A2[:, k, j:j + 1], rstd[:, :n], ALU.mult, ALU.mult,
                        reads=[(xk, k), 'rstd'], writes=[tk])
                    act(h2[:, k, :n], tm[:, :n], AF.Identity, reads=[tk], writes=[(f'h2{s}', k)], bias=modT[:, 24 + k, j:j + 1])

            def stage2(ti):
                t0, n = tiles[ti]
                s = ti % 2
                h2 = h2s[s]
                h2k = [(f'h2{s}', k) for k in range(8)]
                for jb in range(11):
                    ws = wcount[0] % 2
                    wcount[0] += 1
                    wb, wk = wfis[ws], f'wfi{ws}'
                    for gu in range(2):
                        S.dma('pool', wb[:, :, gu, :], wfi_v[:, :, gu, jb * 256:(jb + 1) * 256], writes=[wk + 'gu'[gu]])
                    if ti == 0:
                        for jj in (jb * 2, jb * 2 + 1):
                            S.dma('pool', wfo[:, jj, :], wfo_v[:, jj, :], writes=[('wfo', jj)])
                    for c in range(2):
                        jj = jb * 2 + c
                        bg, bu = nb(0, 7), nb(0, 7)
                        S.mm([(PS[:, bg, :n], wb[:, k, 0, c * 128:(c + 1) * 128], h2[:, k, :n], k == 0, k == 7) for k in range(8)],
                             reads=[wk + 'g'] + h2k, writes=[f'ps{bg}'])
                        S.mm([(PS[:, bu, :n], wb[:, k, 1, c * 128:(c + 1) * 128], h2[:, k, :n], k == 0, k == 7) for k in range(8)],
                             reads=[wk + 'u'] + h2k, writes=[f'ps{bu}'])
                        sg, sk = sgs[jj % 2], f'sg{jj % 2}'
                        act(sg[:, :n], PS[:, bg, :n], AF.Silu, reads=[f'ps{bg}'], writes=[sk])
                        tt('dve', u[:, jj, :n], PS[:, bu, :n], sg[:, :n], ALU.mult, reads=[f'ps{bu}', sk], writes=[('u', jj)])

            def stage3(ti):
                t0, n = tiles[ti]
                j = 0 if ti < 4 else 1
                x1, xk = xof(ti)
                xkeys = [(xk, m) for m in range(8)]
                uk = [('u', jj) for jj in range(22)]
                for m in range(8):
                    b = nb(0, 7)
                    S.mm([(PS[:, b, :n], wfo[:, jj, m * 128:(m + 1) * 128], u[:, jj, :n], jj == 0, jj == 21) for jj in range(22)],
                         reads=[('wfo', jj) for jj in range(22)] + uk, writes=[f'ps{b}'])
                    stt('dve', x1[:, m, :n], PS[:, b, :n], modT[:, 40 + m, j:j + 1], x1[:, m, :n], ALU.mult, ALU.add,
                        reads=[f'ps{b}', (xk, m)], writes=[(xk, m)])
                    if last:
                        act(sq[:, m, :n], x1[:, m, :n], AF.Square, reads=[(xk, m)], writes=[('sq', m)])
                        if m > 0:
                            ss_mm(m - 1, n)
                if not last:
                    if ti < 4:
                        S.dma('sp', xs_v[:, :, t0:t0 + n], x1[:, :, :n], reads=xkeys, writes=[('xs', ti)])
                else:
                    ss_mm(7, n)
                    rstd_fin(n)
                    for k in range(8):
                        stt('dve', sq[:, k, :n], x1[:, k, :n], vecs[:, l, V_FIN + k:V_FIN + k + 1], rstd[:, :n], ALU.mult, ALU.mult,
                            reads=[(xk, k), 'rstd'], writes=[('sq', k)])
                    S.dma('sp', outT_v[:, :, t0:t0 + n], sq[:, :, :n], reads=[('sq', k) for k in range(8)], writes=[('out', ti)])

            S.dma('pool', wo[:, :, 0:256], wov[:, :, 0:256], writes=[('wo', 0)])
            load_ffn_tile(0)
            for q in range(1, 4):
                S.dma('pool', wo[:, :, q * 256:(q + 1) * 256], wov[:, :, q * 256:(q + 1) * 256], writes=[('wo', q)])
            if NTL > 1:
                load_ffn_tile(1)
            stage1(0)
            for ti in range(NTL):
                stage2(ti)
                if ti + 1 < NTL:
                    stage1(ti + 1)
                stage3(ti)
                if ti + 2 < NTL:
                    load_ffn_tile(ti + 2)

    def phase_swa(l, hT, need_ctx):
        with ExitStack() as es:
            wqk = T_(es, [128, 8, 1152], BF16)
            wqks = T_(es, [128, 8, 1152], BF16)
            wv = T_(es, [128, 8, 128], BF16)
            wk2 = T_(es, [128, 8, 128], BF16)
            wk2s = T_(es, [128, 8, 128], BF16)
            rope = T_(es, [128, 2, T], F32)
            trif = T_(es, [128, 2, 128], F32)
            msk = T_(es, [128, 2, 128], BF16)
            esink = T_(es, [128, 8], F32)
            OE = T_(es, [128, 128], BF16)
            OO = T_(es, [128, 128], BF16)
            kT2 = T_(es, [128, TT], BF16)
            VE = T_(es, [128, 18, 128], BF16)
            VO = T_(es, [128, 18, 128], BF16)
            qT = T_(es, [128, 4, TT], BF16)
            tmpA = T_(es, [128, 512], F32)
            tmpB = T_(es, [128, 512], F32)
            pts = [T_(es, [128, 2, 512], BF16) for _ in range(10)]
            rd = T_(es, [128, 4, 128], F32)
            ots = [T_(es, [128, 4, 512], BF16) for _ in range(2)]
            S.dma('pool', wv[:], w_in_v[l][:, :, OFF_VS:OFF_VS + 128], writes=['wv'])
            S.dma('pool', wqk[:, :, 1024:1152], w_in_v[l][:, :, OFF_QS + 1024:OFF_QS + 1152], writes=['wqk_k'])
            for q in range(4):
                S.dma('pool', wqk[:, :, q * 256:(q + 1) * 256], w_in_v[l][:, :, OFF_QS + q * 256:OFF_QS + (q + 1) * 256], writes=[('wqk', q)])
            S.dma('sp', rope[:], rope_in.ap(), writes=['rope'])
            S.dma('sp', trif[:], tri_in[:, 4:6, :], writes=['trif'])
            tcopy('dve', msk[:], trif[:], reads=['trif'], writes=['msk'])
            act(esink[:], vecs[:, l, V_SINK:V_SINK + 8], AF.Exp, reads=[], writes=['esink'])
            S.op('dve', lambda e: e.memset(OE[:], 0.0), writes=['OE'])
            S.op('dve', lambda e: e.memset(OO[:], 0.0), writes=['OO'])
            S.op('dve', lambda e: e.memset(OE[:, 0:64], 1.0), reads=['OE'], writes=['OE'])
            S.op('dve', lambda e: e.memset(OO[:, 64:128], 1.0), reads=['OO'], writes=['OO'])
            S.op('pool', lambda e: e.memset(VE[:], 0.0), writes=['VE'])
            S.op('pool', lambda e: e.memset(VO[:], 0.0), writes=['VO'])
            w4 = wqk[:].rearrange("p k (h d) -> p k h d", d=64)
            w4s = wqks[:].rearrange("p k (h d) -> p k h d", d=64)
            def swap_heads(h0, h1, rkey, wkey):
                ks = []
                for b in range(2):
                    for hf in range(2):
                        d0, s0 = b * 32 + hf * 16, b * 32 + (1 - hf) * 16
                        tcopy('pool', w4s[:, :, h0:h1, d0:d0 + 16], w4[:, :, h0:h1, s0:s0 + 16], reads=[rkey], writes=[(wkey, b, hf)])
                        ks.append((wkey, b, hf))
                return ks
            wsk_k = swap_heads(16, 18, 'wqk_k', 'wqks_k')
            wsk_q = [swap_heads(q * 4, (q + 1) * 4, ('wqk', q), ('wqks', q)) for q in range(4)]
            hk = lambda ti: [('hT', ti, k) for k in range(8)]
            qtiles = list(enumerate(TILES[:4])) + ([(4, TILES[4])] if need_ctx else [])
            import os
            stage = int(os.environ.get("KSTAGE", "9"))
            if stage < 1:
                S.barrier()
                return
            for g in range(2):
                kc0 = 1024 + g * 64
                for hf in range(2):
                    tcopy('pool', wk2[:, :, hf * 64:(hf + 1) * 64], wqk[:, :, kc0:kc0 + 64], reads=['wqk_k'], writes=[f'wk2{hf}'])
                    tcopy('pool', wk2s[:, :, hf * 64:(hf + 1) * 64], wqks[:, :, kc0:kc0 + 64], reads=wsk_k, writes=[f'wk2s{hf}'])
                for g0 in range(0, 18, 8):
                    ng = min(8, 18 - g0)
                    b = nb(0, 4)
                    S.mm([(PS[:, b, a * 64:(a + 1) * 64], hT[:, k, (g0 + a) * 128:(g0 + a + 1) * 128], wv[:, k, g * 64:(g + 1) * 64], k == 0, k == 7)
                          for a in range(ng) for k in range(8)], reads=['wv'], writes=[f'ps{b}'])
                    pv = PS[:, b, 0:ng * 64].rearrange("p (a d) -> p a d", d=64)
                    tcopy('dve', VE[:, g0:g0 + ng, 0:64], pv, reads=[f'ps{b}', 'VE'], writes=[('VE', g0)])
                    tcopy('dve', VO[:, g0:g0 + ng, 64:128], pv, reads=[f'ps{b}', 'VO'], writes=[('VO', g0)])
                for ti, (t0, n) in enumerate(TILES):
                    b = nb(0, 4)
                    S.mm([(PS[:, b, :n], wk2[:, k, :], hT[:, k, t0:t0 + n], k == 0, k == 7) for k in range(8)],
                         reads=['wk20', 'wk21'] + hk(ti), writes=[f'ps{b}'])
                    if ti < 4:
                        b2 = nb(0, 4)
                        S.mm([(PS[:, b2, :n], wk2s[:, k, :], hT[:, k, t0:t0 + n], k == 0, k == 7) for k in range(8)],
                             reads=['wk2s0', 'wk2s1'] + hk(ti), writes=[f'ps{b2}'])
                        rope_apply(kT2[:, t0:t0 + n], PS[:, b, :n], PS[:, b2, :n], rope, 128, t0, n, tmpA, tmpB,
                                   [f'ps{b}', f'ps{b2}', 'rope'], ('kT2', ti))
                    else:
                        act(kT2[:, t0:t0 + n], PS[:, b, :n], AF.Copy, reads=[f'ps{b}'], writes=[('kT2', ti)])
                for pp in range(4):
                    p = g * 4 + pp
                    for ti, (t0, n) in qtiles:
                        b = nb(0, 4)
                        S.mm([(PS[:, b, :n], wqk[:, k, p * 128:(p + 1) * 128], hT[:, k, t0:t0 + n], k == 0, k == 7) for k in range(8)],
                             reads=[('wqk', p // 2)] + hk(ti), writes=[f'ps{b}'])
                        if ti < 4:
                            b2 = nb(0, 4)
                            S.mm([(PS[:, b2, :n], wqks[:, k, p * 128:(p + 1) * 128], hT[:, k, t0:t0 + n], k == 0, k == 7) for k in range(8)],
                                 reads=wsk_q[p // 2] + hk(ti), writes=[f'ps{b2}'])
                            rope_apply(qT[:, pp, t0:t0 + n], PS[:, b, :n], PS[:, b2, :n], rope, 128, t0, n, tmpA, tmpB,
                                       [f'ps{b}', f'ps{b2}', 'rope'], ('qT', pp, ti))
                        else:
                            act(qT[:, pp, t0:t0 + n], PS[:, b, :n], AF.Copy, reads=[f'ps{b}'], writes=[('qT', pp, ti)])
                if stage < 2:
                    continue
                nblk = 18 if need_ctx else 16

                def blk_chunks(i):
                    if i < 16:
                        ch = [(jj, (0 if jj == i - 1 else (1 if jj == i + 1 else None))) for jj in (i - 1, i, i + 1) if 0 <= jj < 16]
                        return ch + [(16, None), (17, None)]
                    return [(16, None), (17, None)]

                def score_chunk(i, ci):
                    q0 = i * 128
                    qk = [('qT', pp, q0 // 512) for pp in range(4)]
                    jj, mk = blk_chunks(i)[ci]
                    bA = 2 * (bankc['i'] % 2)
                    bankc['i'] += 1
                    kk = ('kT2', jj // 4)
                    S.mm([(PS[:, bA, :], kT2[0:64, jj * 128:(jj + 1) * 128], qT[0:64, :, q0:q0 + 128], True, True)],
                         reads=[kk] + qk, writes=[f'ps{bA}'])
                    S.mm([(PS[:, bA + 1, :], kT2[64:128, jj * 128:(jj + 1) * 128], qT[64:128, :, q0:q0 + 128], True, True)],
                         reads=[kk] + qk, writes=[f'ps{bA + 1}'])
                    sl = (i % 2) * 5 + ci
                    pt, pk = pts[sl], f'pt{sl}'
                    act(pt[:], PS[:, bA:bA + 2, :], AF.Exp, reads=[f'ps{bA}', f'ps{bA + 1}'], writes=[pk], scale=0.125)
                    if mk is not None:
                        pv = pt[:].rearrange("p e (a q) -> p (e a) q", q=128)
                        tt('pool', pv, pv, msk[:, mk, :].unsqueeze(1).to_broadcast([128, 8, 128]), ALU.mult,
                           reads=[pk, 'msk'], writes=[pk])

                def pv_pair(i, pp):
                    chunks = blk_chunks(i)
                    nch = len(chunks)
                    par = i % 2
                    ob, db = 4 + par, 6 + par
                    pks = [f'pt{(i % 2) * 5 + ci}' for ci in range(nch)]
                    vk = sorted(set([('VE', (jj // 8) * 8) for jj, _ in chunks] + [('VO', (jj // 8) * 8) for jj, _ in chunks]), key=str) + ['OE', 'OO']
                    cs = slice(pp * 128, (pp + 1) * 128)
                    for bank, (LE, LO) in ((ob, (None, None)), (db, (OE, OO))):
                        items = []
                        for ci, (jj, mk) in enumerate(chunks):
                            pt = pts[(i % 2) * 5 + ci]
                            le = VE[:, jj, :] if LE is None else LE[:]
                            lo = VO[:, jj, :] if LO is None else LO[:]
                            items.append((PS[:, bank, cs], le, pt[:, 0, cs], ci == 0, False))
                            items.append((PS[:, bank, cs], lo, pt[:, 1, cs], False, ci == nch - 1))
                        if pp == 0:
                            S.mm(items, reads=pks + vk, writes=[f'ps{bank}'])
                        else:
                            S.mm(items, reads=pks + vk, cont=[f'ps{bank}'])

                def pv_finish(i):
                    par = i % 2
                    ob, db = 4 + par, 6 + par
                    tt('dve', rd[:], PS[:, db, :].rearrange("p (a q) -> p a q", q=128),
                       esink[:, g * 4:(g + 1) * 4].unsqueeze(2).to_broadcast([128, 4, 128]), ALU.add,
                       reads=[f'ps{db}', 'esink'], writes=['rd'])
                    recip(rd[:], rd[:], reads=['rd'], writes=['rd'])
                    og = (i // 4) % 2
                    ot = ots[og]
                    oc = (i % 4) * 128
                    tt('dve', ot[:, :, oc:oc + 128], PS[:, ob, :].rearrange("p (a q) -> p a q", q=128), rd[:], ALU.mult,
                       reads=[f'ps{ob}', 'rd'], writes=[(f'ot{og}', i % 4)])
                    if i % 4 == 3 or i == nblk - 1:
                        nq = (i % 4 + 1) * 128
                        qg0 = (i // 4) * 512
                        for pp in range(4):
                            S.dma('sp', yD_v[:, g * 4 + pp, qg0:qg0 + nq], ot[:, pp, 0:nq],
                                  reads=[(f'ot{og}', a) for a in range(i % 4 + 1)], writes=[('yD', g, pp, i)])

                def emit_block(i, prev):
                    pairs_left = list(range(4)) if prev is not None else []
                    nci = len(blk_chunks(i)) if i is not None else 0
                    for ci in range(nci):
                        score_chunk(i, ci)
                        if pairs_left:
                            pv_pair(prev, pairs_left.pop(0))
                    while pairs_left:
                        pv_pair(prev, pairs_left.pop(0))
                    if prev is not None:
                        pv_finish(prev)

                for i in range(nblk):
                    emit_block(i, i - 1 if i > 0 else None)
                emit_block(None, nblk - 1)

    def phase_gla(l, hT, need_ctx):
        gw = [w_gkf, w_gkb]
        gb = [b_gkf, b_gkb]
        with ExitStack() as es:
            wg = T_(es, [128, 8, 32], BF16)
            gaug = T_(es, [32, 2, TT], BF16)
            wgk = T_(es, [32, 2, 512], BF16)
            tri = T_(es, [128, 4, 128], F32)
            mk = T_(es, [128, 2, 128], BF16)
            S.dma('pool', wg[:], w_in_v[l][:, :, OFF_GKF:OFF_GKF + 32], writes=['wg'])
            S.dma('sp', tri[:], tri_in[:, 0:4, :], writes=['tri'])
            tcopy('dve', mk[:], tri[:, 0:2, :], reads=['tri'], writes=['mk'])
            trib = T_(es, [128, 4, 128], BF16)
            ones256b = T_(es, [128, 128], BF16)
            tcopy('dve', trib[:], tri[:], reads=['tri'], writes=['trib'])
            S.op('dve', lambda e: e.memset(ones256b[:], 1.0 / 256), writes=['o256b'])
            S.op('dve', lambda e: e.memset(gaug[:], 1.0), writes=['gaug'])
            for d in range(2):
                S.dma('pool', wgk[0:16, d, :], gw[d][l], writes=[f'wgk{d}'])
                S.dma('pool', wgk[16:17, d, :], gb[d][l].unsqueeze(0), writes=[f'wgkb{d}'])
            hk = lambda ti: [('hT', ti, k) for k in range(8)]
            for ti, (t0, n) in enumerate(TILES):
                for d in range(2):
                    b = nb(0, 4)
                    S.mm([(PS[0:16, b, :n], wg[:, k, d * 16:(d + 1) * 16], hT[:, k, t0:t0 + n], k == 0, k == 7) for k in range(8)],
                         reads=['wg'] + hk(ti), writes=[f'ps{b}'])
                    act(gaug[0:16, d, t0:t0 + n], PS[0:16, b, :n], AF.Copy, reads=[f'ps{b}', 'gaug'], writes=[('gaug', d, ti)])
            import os
            gstage = int(os.environ.get("GSTAGE", "9"))
            for grp in range(2):
                if gstage < 1:
                    continue
                with ExitStack() as es2:
                    wq2 = T_(es2, [128, 8, 256], BF16)
                    wk2 = T_(es2, [128, 8, 256], BF16)
                    wv2 = T_(es2, [128, 8, 512], BF16)
                    wga = T_(es2, [128, 8, 512], BF16)
                    qT = T_(es2, [128, 2, TT], BF16)
                    kT = T_(es2, [128, 2, TT], BF16)
                    ktm = T_(es2, [128, 18, 256], BF16)
                    vtm = T_(es2, [128, 18, 512], BF16)
                    obw = T_(es2, [128, 4, TT], BF16)
                    st = [T_(es2, [128, 256], F32) for _ in range(2)]
                    sbf = [T_(es2, [128, 256], BF16) for _ in range(2)]
                    tst = [T_(es2, [128, 256], F32) for _ in range(2)]
                    exs = [T_(es2, [128, 256], F32) for _ in range(2)]
                    nls = [T_(es2, [128, 256], F32) for _ in range(2)]
                    E13s = [T_(es2, [128, 512], F32) for _ in range(3)]
                    E2s = [T_(es2, [128, 2, 128], F32) for _ in range(3)]
                    nhs = [T_(es2, [128, 256], BF16) for _ in range(2)]
                    nlos = [T_(es2, [128, 256], BF16) for _ in range(2)]
                    qds = [T_(es2, [128, 2, 128], BF16) for _ in range(3)]
                    kis = [T_(es2, [128, 2, 128], BF16) for _ in range(3)]
                    kes = [T_(es2, [128, 256], BF16) for _ in range(3)]
                    sTs = [T_(es2, [128, 2, 128], BF16) for _ in range(3)]
                    wq2f = wq2.bitcast(F32)
                    wk2f = wk2.bitcast(F32)
                    osums = [T_(es2, [128, 4, 128], F32), wq2f[:, 0:4, :], wq2f[:, 4:8, :]]
                    osqs = [T_(es2, [128, 4, 128], BF16), wk2[:, 0:2, :].rearrange("p a (b c) -> p (a b) c", c=128)]
                    rss = [T_(es2, [128, 2, 128], F32), wk2f[:, 4:6, :]]
                    sga = T_(es2, [128, 4, TT], BF16)
                    yts = [T_(es2, [128, 4, 512], BF16) for _ in range(2)]
                    g2 = grp * 2
                    S.dma('pool', wq2[:], w_in_v[l][:, :, OFF_QA + g2 * 128:OFF_QA + g2 * 128 + 256], writes=['wq2'])
                    S.dma('pool', wk2[:], w_in_v[l][:, :, OFF_KA + g2 * 128:OFF_KA + g2 * 128 + 256], writes=['wk2'])
                    S.dma('pool', wv2[:], w_in_v[l][:, :, OFF_VA + g2 * 256:OFF_VA + g2 * 256 + 512], writes=['wv2'])
                    S.dma('pool', wga[:], w_in_v[l][:, :, OFF_GA + g2 * 256:OFF_GA + g2 * 256 + 512], writes=['wga'])
                    for ti, (t0, n) in enumerate(TILES):
                        for hh in range(2):
                            b = nb(0, 4)
                            S.mm([(PS[:, b, :n], wq2[:, k, hh * 128:(hh + 1) * 128], hT[:, k, t0:t0 + n], k == 0, k == 7) for k in range(8)],
                                 reads=['wq2'] + hk(ti), writes=[f'ps{b}'])
                            act(qT[:, hh, t0:t0 + n], PS[:, b, :n], AF.Copy, reads=[f'ps{b}'], writes=[('qT', hh, ti)], scale=128.0 ** -0.5)
                            b = nb(0, 4)
                            S.mm([(PS[:, b, :n], wk2[:, k, hh * 128:(hh + 1) * 128], hT[:, k, t0:t0 + n], k == 0, k == 7) for k in range(8)],
                                 reads=['wk2'] + hk(ti), writes=[f'ps{b}'])
                            tcopy('dve', kT[:, hh, t0:t0 + n], PS[:, b, :n], reads=[f'ps{b}'], writes=[('kT', hh, ti)])
                    for t8 in range(18):
                        t128 = t8 * 128
                        b = nb(0, 4)
                        S.mm([(PS[:, b, 0:256], hT[:, k, t128:t128 + 128], wk2[:, k, :], k == 0, k == 7) for k in range(8)],
                             reads=['wk2'] + hk(t128 // 512), writes=[f'ps{b}'])
                        act(ktm[:, t8, :], PS[:, b, 0:256], AF.Copy, reads=[f'ps{b}'], writes=[('ktm', t8)])
                        b = nb(0, 4)
                        S.mm([(PS[:, b, :], hT[:, k, t128:t128 + 128], wv2[:, k, :], k == 0, k == 7) for k in range(8)],
                             reads=['wv2'] + hk(t128 // 512), writes=[f'ps{b}'])
                        tcopy('dve', vtm[:, t8, :], PS[:, b, :], reads=[f'ps{b}'], writes=[('vtm', t8)])
                    for ti, (t0, n) in enumerate(TILES if need_ctx else TILES[:4]):
                        for a in range(4):
                            b = nb(0, 4)
                            S.mm([(PS[:, b, :n], wga[:, k, a * 128:(a + 1) * 128], hT[:, k, t0:t0 + n], k == 0, k == 7) for k in range(8)],
                                 reads=['wga'] + hk(ti), writes=[f'ps{b}'])
                            act(sga[:, a, t0:t0 + n], PS[:, b, :n], AF.Silu, reads=[f'ps{b}'], writes=[('sga', a, ti)])
                    S.barrier()
                    for d in (1, 0):
                        order = [16, 17] + list(range(16)) if d == 0 else [17, 16] + list(range(15, -1, -1))
                        corder = (0, 1) if d == 0 else (1, 0)
                        for hh in range(2):
                            S.op('dve', lambda e, hh=hh: e.memset(st[hh][:], 0.0), reads=[('st', hh)], writes=[('st', hh)])
                            S.op('dve', lambda e, hh=hh: e.memset(sbf[hh][:], 0.0), reads=[('sbf', hh)], writes=[('sbf', hh)])

                        def preA(t8, sl, d=d):
                            tsl = slice(t8 * 128, t8 * 128 + 128)
                            ti = (t8 * 128) // 512
                            K = lambda nm: f'{nm}A{sl}'
                            S.mm([(PS[:, 0, 0:256], gaug[0:17, d, tsl], wgk[0:17, d, grp * 256:(grp + 1) * 256], True, True)],
                                 reads=[('gaug', d, ti), 'gaug', f'wgk{d}', f'wgkb{d}'], writes=['ps0'])
                            act(exs[sl][:], PS[:, 0, 0:256], AF.Exp, reads=['ps0'], writes=[K('ex')], scale=-1.0)
                            act(nls[sl][:], exs[sl][:], AF.Ln, reads=[K('ex')], writes=[K('nl')], bias=1.0)
                            tcopy('pool', nhs[sl][:], nls[sl][:], reads=[K('nl')], writes=[K('nh')])
                            tt('pool', nlos[sl][:], nls[sl][:], nhs[sl][:], ALU.subtract, reads=[K('nl'), K('nh')], writes=[K('nlo')])

                        def preB(t8, sl, sa, d=d):
                            tsl = slice(t8 * 128, t8 * 128 + 128)
                            ti = (t8 * 128) // 512
                            nh, nlo, E13, E2, qd, ki, ke = nhs[sa], nlos[sa], E13s[sl], E2s[sl], qds[sl], kis[sl], kes[sl]
                            K = lambda nm: (f'{nm}A{sa}' if nm in ('nh', 'nlo') else f'{nm}{sl}')
                            items = []
                            for hh in range(2):
                                hs = slice(hh * 128, (hh + 1) * 128)
                                items.append((PS[:, 1, hs], nh[:, hs], trib[:, 2 * d, :], True, False))
                                items.append((PS[:, 1, hs], nlo[:, hs], trib[:, 2 * d, :], False, True))
                            items.append((PS[:, 1, 256:512], trib[:, 2 * d + 1, :], nh[:], True, False))
                            items.append((PS[:, 1, 256:512], trib[:, 2 * d + 1, :], nlo[:], False, True))
                            S.mm(items, reads=[K('nh'), K('nlo'), 'trib'], writes=['ps1'])
                            act(E13[:], PS[:, 1, :], AF.Exp, reads=['ps1'], writes=[K('E1')], scale=-1.0 / 16)
                            act(E2[:], PS[:, 1, 0:256].rearrange("p (h c) -> p h c", c=128), AF.Exp, reads=['ps1'], writes=[K('E2')], scale=1.0 / 16)
                            tt('dve', qd[:], qT[:, :, tsl], E13[:, 0:256].rearrange("p (h c) -> p h c", c=128), ALU.mult,
                               reads=[('qT', 0, ti), ('qT', 1, ti), K('E1')], writes=[K('qd')])
                            tt('pool', ki[:], kT[:, :, tsl], E2[:], ALU.mult, reads=[('kT', 0, ti), ('kT', 1, ti), K('E2')], writes=[K('ki')])
                            tt('pool', ke[:], ktm[:, t8, :], E13[:, 256:512], ALU.mult, reads=[('ktm', t8), K('E1')], writes=[K('ke')])

                        def preC(t8, sl, d=d):
                            K = lambda nm: f'{nm}{sl}'
                            S.mm([(PS[:, 3, hh * 128:(hh + 1) * 128], kis[sl][:, hh, :], qds[sl][:, hh, :], True, True) for hh in range(2)],
                                 reads=[K('ki'), K('qd')], writes=['ps3'])
                            tt('dve', sTs[sl][:], PS[:, 3, 0:256].rearrange("p (h c) -> p h c", c=128),
                               mk[:, d, :].unsqueeze(1).to_broadcast([128, 2, 128]), ALU.mult, reads=['ps3', 'mk'], writes=[K('sT')])

                        def chunk_id(sl, d=d):
                            E13 = E13s[sl]
                            dcol = 127 if d == 0 else 0
                            for hh in range(2):
                                act(tst[hh][:], st[hh][:], AF.Identity, reads=[('st', hh), f'E1{sl}'], writes=[('tst', hh)],
                                    scale=E13[:, hh * 128 + dcol:hh * 128 + dcol + 1])

                        def chunk(t8, sl, ob, nsl, d=d):
                            qd, ke, sT = qds[sl], kes[sl], sTs[sl]
                            K = lambda nm: f'{nm}{sl}'
                            for hh in range(2):
                                items = []
                                for jv in range(2):
                                    oc = (hh * 2 + jv) * 128
                                    vc = slice(hh * 256 + jv * 128, hh * 256 + (jv + 1) * 128)
                                    items.append((PS[:, ob, oc:oc + 128], vtm[:, t8, vc], sT[:, hh, :], True, False))
                                    items.append((PS[:, ob, oc:oc + 128], sbf[hh][:, jv * 128:(jv + 1) * 128], qd[:, hh, :], False, True))
                                S.mm(items, reads=[('vtm', t8), K('sT'), ('sbf', hh), K('qd')], writes=[(f'ps{ob}', hh)])
                                S.mm([(PS[:, 5 + hh, 0:256], ke[:, hh * 128:(hh + 1) * 128], vtm[:, t8, hh * 256:(hh + 1) * 256], True, True)],
                                     reads=[K('ke'), ('vtm', t8)], writes=[('ps5', hh)])
                            for hh in range(2):
                                tt('dve', st[hh][:], PS[:, 5 + hh, 0:256], tst[hh][:], ALU.add,
                                   reads=[('tst', hh), ('ps5', hh)], writes=[('st', hh)])
                            for hh in range(2):
                                act(sbf[hh][:], st[hh][:], AF.Copy, reads=[('st', hh)], writes=[('sbf', hh)])
                            if nsl is not None:
                                chunk_id(nsl)

                        def evac1(t8, ob, n, d=d):
                            tsl = slice(t8 * 128, t8 * 128 + 128)
                            p4k = [(f'ps{ob}', hh) for hh in range(2)]
                            p4 = PS[:, ob, :].rearrange("p (a q) -> p a q", q=128)
                            if d == 1:
                                act(obw[:, :, tsl], p4, AF.Copy, reads=p4k, writes=[('obw', t8)])
                                return False
                            if not (t8 < 16 or need_ctx):
                                return False
                            osum, osq = osums[n % 3], osqs[n % 2]
                            tt('dve', osum[:], p4, obw[:, :, tsl], ALU.add, reads=p4k + [('obw', t8)], writes=[f'osum{n % 3}'])
                            act(osq[:], osum[:], AF.Square, reads=[f'osum{n % 3}'], writes=[f'osq{n % 2}'])
                            return True

                        def evac2a(t8, n):
                            osq, rs = osqs[n % 2], rss[n % 2]
                            S.mm([(PS[:, 0, 256 + hh * 128:256 + (hh + 1) * 128], ones256b[:], osq[:, hh * 2 + jv, :], jv == 0, jv == 1)
                                  for hh in range(2) for jv in range(2)], reads=[f'osq{n % 2}', 'o256b'], writes=['ps0'])
                            act(rs[:], PS[:, 0, 256:512].rearrange("p (h c) -> p h c", c=128), AF.Ln, reads=['ps0'], writes=[f'rs{n % 2}'], bias=EPS)
                            act(rs[:], rs[:], AF.Exp, reads=[f'rs{n % 2}'], writes=[f'rs{n % 2}'], scale=-0.5)

                        def evac2b(t8, n):
                            tsl = slice(t8 * 128, t8 * 128 + 128)
                            ti = (t8 * 128) // 512
                            osum, rs = osums[n % 3], rss[n % 2]
                            ok_ = f'osum{n % 3}'
                            o4 = osum[:].rearrange("p (h j) q -> p h j q", j=2)
                            tt('dve', o4, o4, rs[:].unsqueeze(2).to_broadcast([128, 2, 2, 128]), ALU.mult, reads=[ok_, f'rs{n % 2}'], writes=[ok_])
                            yg = (t8 // 4) % 2
                            yt = yts[yg]
                            yc = (t8 % 4) * 128
                            for jv in range(2):
                                stt('dve', yt[:, :, yc:yc + 128].rearrange("p (h j) q -> p j h q", j=2)[:, jv],
                                    osum[:].rearrange("p (h j) q -> p j h q", j=2)[:, jv], vecs[:, l, V_GN + jv:V_GN + jv + 1],
                                    sga[:, :, tsl].rearrange("p (h j) q -> p j h q", j=2)[:, jv], ALU.mult, ALU.mult,
                                    reads=[ok_] + [('sga', a, ti) for a in range(4)], writes=[(f'yt{yg}', t8 % 4, jv), (f'yt{yg}', t8 % 4, jv + 2)])
                            if t8 % 4 == 3 or t8 == 17:
                                nq = (t8 % 4 + 1) * 128
                                tg0 = (t8 // 4) * 512
                                for a in range(4):
                                    S.dma('sp', yD_v[:, grp * 4 + a, tg0:tg0 + nq], yt[:, a, 0:nq],
                                          reads=[(f'yt{yg}', q, a) for q in range(t8 % 4 + 1)], writes=[('yD', grp, a, t8)])

                        NT = len(order)
                        preA(order[0], 0)
                        preA(order[1], 1)
                        preB(order[0], 0, 0)
                        preA(order[2], 0)
                        preB(order[1], 1, 1)
                        preC(order[0], 0)
                        pa = pb = None
                        chunk_id(0)
                        for n, t8 in enumerate(order):
                            ob = 4 if n % 2 == 0 else 7
                            if n + 1 < NT:
                                preC(order[n + 1], (n + 1) % 3)
                            chunk(t8, n % 3, ob, ((n + 1) % 3) if n + 1 < NT else None)
                            if n + 2 < NT:
                                preB(order[n + 2], (n + 2) % 3, (n + 2) % 2)
                            if pb is not None:
                                evac2b(*pb)
                                pb = None
                            if pa is not None:
                                evac2a(*pa)
                                pa, pb = None, pa
                            if n + 3 < NT:
                                preA(order[n + 3], (n + 3) % 2)
                            if evac1(t8, ob, n):
                                pa = (t8, n)
                        if pb is not None:
                            evac2b(*pb)
                        if pa is not None:
                            evac2a(*pa)
                            evac2b(*pa)
                    S.barrier()

    import os
    only = os.environ.get("KPHASES", "mod,norm,gla,m0,swa,m1,mla,m2,ffn").split(",")
    n_layers = int(os.environ.get("KLAYERS", n_layers))
    for l in range(n_layers):
        last = l == NL - 1
        need_ctx = not last
        if 'mod' in only and l == 0:
            phase_mod(l)
        S.barrier()
        with ExitStack() as esl:
            hT = T_(esl, [128, 8, TT], BF16, "hT")
            if 'norm' in only:
                phase_norm(l, hT)
            S.barrier()
            if 'gla' in only:
                phase_gla(l, hT, need_ctx)
            S.barrier()
            if 'm0' in only:
                phase_merge(l, hT, 0, need_ctx)
            S.barrier()
            if 'swa' in only:
                phase_swa(l, hT, need_ctx)
            S.barrier()
            if 'm1' in only:
                phase_merge(l, hT, 1, need_ctx)
            S.barrier()
            if 'mla' in only:
                phase_mla(l, hT, need_ctx, host_mod=(l + 1 if l + 1 < n_layers else None))
            S.barrier()
            if 'm2' in only:
                phase_merge(l, hT, 2, need_ctx)
            S.barrier()
        if 'ffn' in only:
            phase_ffn(l, need_ctx, last)
        S.barrier()
    for name in dbg_t:
        if name == 'yD':
            with ExitStack() as es:
                tb = T_(es, [128, 8, TT], BF16)
                S.dma('sp', tb[:], yD_v, writes=['tb'])
                S.dma('sp', fm(dbg_t[name].ap()), tb[:], reads=['tb'], writes=['dbg_yD'])
                S.barrier()
        if name == 'mD':
            with ExitStack() as es:
                tb = T_(es, [128, 8, TT], F32)
                S.dma('sp', tb[:], mD_v, writes=['tb'])
                S.dma('sp', fm(dbg_t[name].ap()), tb[:], reads=['tb'], writes=['dbg_mD'])
                S.barrier()
    S.barrier()
    G.close()
    print(f"[build] instructions={S.n_ins} waits={S.n_wait}")
    return nc


def host_consts():
    t = np.arange(T)
    row, col = t // 64, t % 64
    cos = np.zeros((64, T), np.float32)
    sin = np.zeros((64, T), np.float32)
    for f in range(64):
        blk, i = f // 32, f % 32
        half, idx = i // 16, i % 16
        inv = np.float32(10000.0) ** (-np.float32(idx) / np.float32(16))
        pos = (row if blk == 0 else col).astype(np.float32)
        ang = pos * inv
        cos[f] = np.cos(ang)
        sin[f] = np.sin(ang) * (-1.0 if half == 0 else 1.0)
    rope = np.zeros((128, 2, T), np.float32)
    rope[:64, 0], rope[64:, 0] = cos, cos
    rope[:64, 1], rope[64:, 1] = sin, sin
    a = np.arange(128)
    same = (a[:, None] // 64) == (a[None, :] // 64)
    tri = np.zeros((128, 6, 128), np.float32)
    tri[:, 0] = (a[:, None] <= a[None, :])
    tri[:, 1] = (a[:, None] > a[None, :])
    tri[:, 2] = (a[:, None] >= a[None, :])
    tri[:, 3] = (a[:, None] < a[None, :])
    tri[:, 4] = a[None, :] <= a[:, None]
    tri[:, 5] = a[:, None] <= a[None, :]
    return rope, tri


def pack_vecs(inp):
    v = np.zeros((128, NL, NV), np.float32)
    fmv = lambda z: np.ascontiguousarray(np.asarray(z, np.float32).reshape(-1, 128).T)
    for l in range(NL):
        v[:, l, V_BMOD:V_BMOD + 48] = fmv(inp['b_mod'][l])
        v[:, l, V_NMIX:V_NMIX + 8] = fmv(inp['norm_mix'][l])
        v[:, l, V_NFFN:V_NFFN + 8] = fmv(inp['norm_ffn'][l])
        v[:, l, V_FIN:V_FIN + 8] = fmv(inp['final_norm'])
        v[:, l, V_QN:V_QN + 3] = fmv(inp['q_norm'][l])
        v[:, l, V_KVN:V_KVN + 2] = fmv(inp['kv_norm'][l])
        v[:, l, V_GN:V_GN + 2] = fmv(inp['gla_norm'][l])
        sk = np.asarray(inp['sinks'][l], np.float32)
        for j in range(8):
            v[:64, l, V_SINK + j] = sk[2 * j]
            v[64:, l, V_SINK + j] = sk[2 * j + 1]
    return v


WNAMES = ['w_mod', 'w_in', 'w_gk_fwd', 'b_gk_fwd', 'w_gk_bwd', 'b_gk_bwd', 'w_q_up', 'w_kv_up',
          'w_pa', 'w_pb', 'w_pc', 'w_o', 'w_ffn_in', 'w_ffn_out']


def make_in_maps(inp, cores):
    rope, tri = host_consts()
    vecs = pack_vecs(inp)
    shared = {n: np.ascontiguousarray(np.asarray(inp[n], np.float32)) for n in WNAMES}
    shared.update(rope=rope, tri=tri, vecs=vecs)
    x = np.asarray(inp['x'], np.float32)
    ctx = np.asarray(inp['ctx'], np.float32)
    c = np.asarray(inp['c'], np.float32)
    c_ctx = np.asarray(inp['c_ctx'], np.float32)
    maps = []
    for b in cores:
        m = dict(shared)
        m['xT'] = np.ascontiguousarray(x[b].T)
        m['ctxT'] = np.ascontiguousarray(ctx[b].T)
        cc = np.stack([c[b].reshape(8, 128).T, c_ctx.reshape(8, 128).T], axis=-1)
        m['cc'] = np.ascontiguousarray(cc.astype(np.float32))
        maps.append(m)
    return maps


def kernel(**inputs):
    nc = build()
    maps = make_in_maps(inputs, list(range(8)))
    res = run_bass_kernel_spmd(nc, maps, core_ids=list(range(8)))
    out = np.stack([np.ascontiguousarray(r["outT"].T) for r in res.results], axis=0)
    return out.astype(np.float32)
```
